# Optimizing a Trainium2 kernel written in Bass

```python
import math
import jax, jax.numpy as jnp
from jax import lax
import numpy as np

D_MODEL = 1024
BATCH = 8
SEQ = 8192
DEPTH = 1

PLE_DIM = 256
W_CONV = D_MODEL
CONV_GROUPS = 8
CONV_K = 31
HG_HEAD_DIM = 128
W_HGRN = D_MODEL
HG_HEADS = W_HGRN // HG_HEAD_DIM
W_MIX = W_CONV + W_HGRN
CHUNK = 64
EPS = 1e-6
W_IN_COLS = 3 * W_CONV + 4 * W_HGRN

kernel_name = "hymba_conformer_hgrn2_hybrid"


def rms_norm(x, g):
    xf = x.astype(jnp.float32)
    y = xf * lax.rsqrt(jnp.mean(xf * xf, axis=-1, keepdims=True) + EPS)
    return (y * g.astype(jnp.float32)).astype(x.dtype)


def group_layer_norm(x, g, b, n_groups):
    shp = x.shape
    xf = x.astype(jnp.float32).reshape(shp[:-1] + (n_groups, shp[-1] // n_groups))
    mu = jnp.mean(xf, axis=-1, keepdims=True)
    var = jnp.mean(jnp.square(xf - mu), axis=-1, keepdims=True)
    y = ((xf - mu) * lax.rsqrt(var + EPS)).reshape(shp)
    return (y * g.astype(jnp.float32) + b.astype(jnp.float32)).astype(x.dtype)


def conformer_conv_branch(z_val, z_glu, z_gate, conv_w, conv_b, cn_g, cn_b, w_pw2, b_pw2):
    v = z_val * jax.nn.sigmoid(z_glu)
    y = lax.conv_general_dilated(
        v, conv_w[:, None, :].astype(v.dtype),
        window_strides=(1,), padding=[(CONV_K - 1, 0)],
        dimension_numbers=("NWC", "WIO", "NWC"),
        feature_group_count=W_CONV) + conv_b.astype(v.dtype)
    y = jax.nn.silu(group_layer_norm(y, cn_g, cn_b, CONV_GROUPS))
    y = y @ w_pw2.astype(y.dtype) + b_pw2.astype(y.dtype)
    return y * jax.nn.silu(z_gate)


def _gla_chunk_step(S, inp):
    q, k, v, lf = inp
    b = jnp.cumsum(lf, axis=2)
    o_inter = jnp.einsum("bhck,bhkv->bhcv", q * jnp.exp(b), S)
    t_idx = jnp.arange(CHUNK)
    causal = (t_idx[:, None] >= t_idx[None, :])[None, None, :, :, None]
    diff = b[:, :, :, None, :] - b[:, :, None, :, :]
    decay = jnp.exp(jnp.where(causal, diff, -jnp.inf))
    A = jnp.einsum("bhtsk,bhsk->bhts", q[:, :, :, None, :] * decay, k)
    o_intra = jnp.einsum("bhts,bhsv->bhtv", A, v)
    b_last = b[:, :, -1, :]
    k_dec = k * jnp.exp(b_last[:, :, None, :] - b)
    S_new = jnp.exp(b_last)[..., None] * S + jnp.einsum("bhsk,bhsv->bhkv", k_dec, v)
    return S_new, o_inter + o_intra


def hgrn2_branch(zq, zf, zi, zg, lb, onorm_g):
    B, T, _ = zq.shape
    n_chunks = T // CHUNK
    lbf = lb.astype(jnp.float32)
    zf32 = zf.astype(jnp.float32)
    f = lbf + (1.0 - lbf) * jax.nn.sigmoid(zf32)
    log_f = jnp.log(f)
    k = (1.0 - lbf) * jax.nn.sigmoid(-zf32)
    q = jax.nn.silu(zq.astype(jnp.float32))
    v = zi.astype(jnp.float32)

    def to_chunks(t):
        return t.reshape(B, n_chunks, CHUNK, HG_HEADS, HG_HEAD_DIM).transpose(1, 0, 3, 2, 4)

    S0 = jnp.zeros((B, HG_HEADS, HG_HEAD_DIM, HG_HEAD_DIM), jnp.float32)
    _, o = lax.scan(_gla_chunk_step, S0,
                    (to_chunks(q), to_chunks(k), to_chunks(v), to_chunks(log_f)))
    o = o.transpose(1, 0, 3, 2, 4).reshape(B, T, HG_HEADS, HG_HEAD_DIM)
    o = rms_norm(o, onorm_g.reshape(HG_HEADS, HG_HEAD_DIM))
    o = o.reshape(B, T, W_HGRN).astype(zq.dtype)
    return o * jax.nn.silu(zg)


def setup_inputs(seed: int = 0) -> dict:
    key = jax.random.key(seed)
    ks = jax.random.split(key, 20)
    f32 = jnp.float32
    nrm = lambda k, s, sc: (jax.random.normal(k, s, f32) * sc).astype(f32)
    return {
        "x": jax.random.normal(ks[0], (BATCH, SEQ, D_MODEL), f32),
        "p": jax.random.normal(ks[1], (DEPTH, BATCH, SEQ, PLE_DIM), f32),
        "ln_g": 1.0 + nrm(ks[2], (DEPTH, D_MODEL), 0.02),
        "w_in": nrm(ks[3], (DEPTH, D_MODEL, W_IN_COLS), D_MODEL ** -0.5),
        "conv_w": nrm(ks[4], (DEPTH, CONV_K, W_CONV), CONV_K ** -0.5),
        "conv_b": nrm(ks[5], (DEPTH, W_CONV), 0.01),
        "cnorm_g": 1.0 + nrm(ks[6], (DEPTH, W_CONV), 0.02),
        "cnorm_b": nrm(ks[7], (DEPTH, W_CONV), 0.01),
        "w_pw2": nrm(ks[8], (DEPTH, W_CONV, W_CONV), W_CONV ** -0.5),
        "b_pw2": nrm(ks[9], (DEPTH, W_CONV), 0.01),
        "lb_logits": nrm(ks[10], (DEPTH + 1, W_HGRN), 0.1),
        "onorm_g": 1.0 + nrm(ks[11], (DEPTH, W_HGRN), 0.02),
        "w_out": nrm(ks[12], (DEPTH, W_MIX, D_MODEL), W_MIX ** -0.5),
        "pe_norm_g": 1.0 + nrm(ks[13], (DEPTH, D_MODEL), 0.02),
        "w_pg": nrm(ks[14], (DEPTH, D_MODEL, D_MODEL), D_MODEL ** -0.5),
        "w_pp": nrm(ks[15], (DEPTH, PLE_DIM, D_MODEL), PLE_DIM ** -0.5),
        "final_g": 1.0 + nrm(ks[16], (D_MODEL,), 0.02),
    }


def reference(x, p, ln_g, w_in, conv_w, conv_b, cnorm_g, cnorm_b, w_pw2, b_pw2,
              lb_logits, onorm_g, w_out, pe_norm_g, w_pg, w_pp, final_g):
    lbs = jnp.cumsum(jax.nn.softmax(lb_logits.astype(jnp.float32), axis=0), axis=0)
    h = x
    split_pts = [W_CONV, 2 * W_CONV, 3 * W_CONV,
                 3 * W_CONV + W_HGRN, 3 * W_CONV + 2 * W_HGRN, 3 * W_CONV + 3 * W_HGRN]
    for i in range(DEPTH):
        u = rms_norm(h, ln_g[i])
        z = u @ w_in[i].astype(u.dtype)
        c_val, c_glu, c_gate, hq, hf, hi, hg = jnp.split(z, split_pts, axis=-1)
        y_conv = conformer_conv_branch(c_val, c_glu, c_gate, conv_w[i], conv_b[i],
                                       cnorm_g[i], cnorm_b[i], w_pw2[i], b_pw2[i])
        y_hgrn = hgrn2_branch(hq, hf, hi, hg, lbs[i], onorm_g[i])
        y = jnp.concatenate([y_conv, y_hgrn.astype(y_conv.dtype)], axis=-1)
        h = h + y @ w_out[i].astype(y.dtype)
        pe = p[i] @ w_pp[i].astype(p.dtype)
        gate = jax.nn.sigmoid(rms_norm(h, pe_norm_g[i]) @ w_pg[i].astype(h.dtype))
        h = h + gate * pe.astype(h.dtype)
    return rms_norm(h, final_g)
```

```python
from contextlib import ExitStack

import numpy as np
import concourse.bass as bass
import concourse.mybir as mybir
from concourse.bass_utils import run_bass_kernel_spmd

F32 = mybir.dt.float32
BF16 = mybir.dt.bfloat16
AF = mybir.ActivationFunctionType
ALU = mybir.AluOpType

D = 1024
NU = 8
TT = 512
NTB = 4
CK = 31
HIST = 32
EPS = 1e-6
NPE_TAPS = 24
NSH = NPE_TAPS - 16
AB = 4
NSLOT = 4 * AB + 20
SLOTW = 4096
NR = 4
SEQ = 8192
NCORES = 8

C_CW = 0
C_CB = 248
C_CG = 256
C_CBETA = 264
C_BPW2 = 272
C_LB0 = 280
C_LB1 = 288
C_ONG = 296
C_PEG = 304
C_FG = 312
NPAR = 320
D_CW = 0
D_A0 = 248
D_A1 = 256
D_NA1 = 264
D_TMP = 272
D_TL = 280
D_EPS = 288
D_CBC = 296
NDPAR = 304
K_ID = 0
K_CM = 128
K_ON = 256
K_O1K = 384
K_AM = 512
K_RM = 640
NCONST = 640 + 512

ENGS = ("pe", "act", "dve", "pool", "sp")


class Node:
    __slots__ = ("idx", "eng", "fns", "cost", "kind", "sem", "tbl", "nbytes", "preds", "succs", "npred",
                 "ready", "start", "finish", "ev")


TBL_SWITCH_NS = 1300.0
DMA_LAT_NS = 2000.0
DMA_BW = 300.0


class Prog:
    def __init__(self):
        self.nodes = []
        self.lastw = {}
        self.readers = {}
        self.join = None
        self.sems = {}

    def _add(self, eng, fns, reads, writes, cost, kind, sem=None, tbl=None, nbytes=0):
        n = Node()
        n.idx = len(self.nodes)
        n.eng, n.fns, n.cost, n.kind, n.sem, n.tbl, n.nbytes = eng, fns, float(cost), kind, sem, tbl, nbytes
        preds = set()
        for k in reads:
            w = self.lastw.get(k)
            if w is not None:
                preds.add(w)
        for k in writes:
            w = self.lastw.get(k)
            if w is not None:
                preds.add(w)
            preds.update(self.readers.get(k, ()))
        if self.join is not None:
            preds.add(self.join)
        preds.discard(n.idx)
        n.preds = preds
        n.succs = []
        self.nodes.append(n)
        for k in reads:
            self.readers.setdefault(k, []).append(n.idx)
        for k in writes:
            self.lastw[k] = n.idx
            self.readers[k] = []
        return n

    def op(self, eng, fn, reads=(), writes=(), cost=600.0, tbl=None):
        return self._add(eng, [fn], reads, writes, cost, "op", tbl=tbl)

    def group(self, eng, fns, reads=(), writes=(), cost=600.0):
        return self._add(eng, list(fns), reads, writes, cost, "op")

    def dma(self, eng, semname, fn, reads=(), writes=(), nbytes=1 << 20):
        return self._add(eng, [fn], reads, writes, 100.0, "dma", sem=semname, nbytes=nbytes)

    def barrier(self, fn):
        n = self._add("dve", [fn], (), (), 100.0, "op")
        n.preds = set(range(n.idx))
        self.join = n.idx

    def schedule(self):
        import heapq
        nodes = self.nodes
        import os as _os
        W = int(_os.environ.get('SCHED_WINDOW', '0'))
        if W:
            for n in nodes:
                if n.idx >= W:
                    n.preds.add(n.idx - W)
        RE = _os.environ.get('SCHED_ENG', 'pe,act,dve')
        if RE is not None:
            allowed = set(RE.split(',')) if RE else set()
            lastn = {}
            for n in nodes:
                if n.eng not in allowed:
                    if n.eng in lastn:
                        n.preds.add(lastn[n.eng])
                    lastn[n.eng] = n.idx
        for n in nodes:
            n.npred = len(n.preds)
            n.ready = 0.0
            for p in n.preds:
                nodes[p].succs.append(n.idx)
        pending = {e: [] for e in ENGS}
        avail = {e: [] for e in ENGS}
        free = {e: 0.0 for e in ENGS}
        cur_tbl = [None]
        dma_free = [0.0]
        for n in nodes:
            if n.npred == 0:
                heapq.heappush(pending[n.eng], (0.0, n.idx))
        order = {e: [] for e in ENGS}
        done = 0
        N = len(nodes)
        import os as _os
        if _os.environ.get('SCHED_INORDER'):
            for n in nodes:
                e = n.eng
                n.start = max(free[e], n.ready)
                free[e] = n.start + n.cost
                n.finish = free[e] + (DMA_LAT_NS if n.kind == 'dma' else 0.0)
                order[e].append(n.idx)
                for s_ in n.succs:
                    m = nodes[s_]
                    if n.finish > m.ready:
                        m.ready = n.finish
            done = N
        while done < N:
            best = None
            for e in ENGS:
                pe_, av = pending[e], avail[e]
                while pe_ and pe_[0][0] <= free[e]:
                    heapq.heappush(av, heapq.heappop(pe_)[1])
                if av:
                    cand = (free[e], 0, e)
                elif pe_:
                    cand = (pe_[0][0], 1, e)
                else:
                    continue
                if best is None or cand < best:
                    best = cand
            start, which, e = best
            if which == 0:
                av = avail[e]
                if e == "act" and len(av) > 1:
                    lo = av[0]
                    pick = None
                    for i in sorted(av)[:24]:
                        t = nodes[i].tbl
                        if t is None or t == cur_tbl[0]:
                            if i - lo < 400:
                                pick = i
                            break
                    if pick is None:
                        pick = lo
                    av.remove(pick)
                    heapq.heapify(av)
                    ni = pick
                else:
                    ni = heapq.heappop(av)
            else:
                ni = heapq.heappop(pending[e])[1]
            n = nodes[ni]
            n.start = start
            if n.kind == "dma":
                free[e] = start + n.cost
                t0 = max(start, dma_free[0])
                dma_free[0] = t0 + n.nbytes / DMA_BW
                n.finish = dma_free[0] + DMA_LAT_NS
            else:
                c = n.cost
                if e == "act" and n.tbl is not None and n.tbl != cur_tbl[0]:
                    c += TBL_SWITCH_NS
                    cur_tbl[0] = n.tbl
                free[e] = start + c
                n.finish = free[e]
            order[e].append(ni)
            done += 1
            for s_ in n.succs:
                m = nodes[s_]
                if n.finish > m.ready:
                    m.ready = n.finish
                m.npred -= 1
                if m.npred == 0:
                    heapq.heappush(pending[m.eng], (m.ready, m.idx))
        self.order = order
        self.makespan = max(n.finish for n in nodes)
        self.busy = {e: sum(nodes[i].cost for i in order[e]) for e in ENGS}
        cnt = {}
        seen = {e: {} for e in ENGS}
        self.prog = {e: [] for e in ENGS}
        glob = sorted(range(N), key=lambda i: (nodes[i].start, i))
        for ni in glob:
            n = nodes[ni]
            e = n.eng
            need = {}
            for p in n.preds:
                k, v = nodes[p].ev
                if e == "pe" and k == "pe":
                    continue
                if need.get(k, 0) < v:
                    need[k] = v
            sn = seen[e]
            for k, v in need.items():
                if sn.get(k, 0) >= v:
                    continue
                sn[k] = v
                self.prog[e].append(("wait", k, v))
            if n.kind == "dma":
                c = cnt.get(n.sem, 0) + 16
                cnt[n.sem] = c
                n.ev = (n.sem, c)
                self.prog[e].append(("op", n.fns[0], n.sem, 16))
            else:
                for fn in n.fns[:-1]:
                    self.prog[e].append(("op", fn, None, 0))
                c = cnt.get(e, 0) + 1
                cnt[e] = c
                n.ev = (e, c)
                self.prog[e].append(("op", n.fns[-1], e, 1))
        for k, v in cnt.items():
            if seen["pool"].get(k, 0) < v:
                self.prog["pool"].append(("wait", k, v))
        self.final_counts = cnt

    def replay(self, eng, engobj):
        for item in self.prog[eng]:
            if item[0] == "wait":
                engobj.wait_ge(self.sems[item[1]], item[2])
            else:
                _, fn, semname, inc = item
                ins = fn(engobj)
                if semname is not None:
                    ins.then_inc(self.sems[semname], inc)


def slot_pieces(s):
    def inproj(seg, j, q):
        return ("w_in", 8, seg * 1024 + j * 128, 128, q * 1024)
    if s < 4 * AB:
        jp, t = divmod(s, AB)
        if t == 0:
            return [inproj(0, 2 * jp, 0), inproj(1, 2 * jp, 1), inproj(0, 2 * jp + 1, 2), inproj(1, 2 * jp + 1, 3)]
        if t == 2:
            return ("convs", jp)
        return ("conv", 2 * jp + (1 if t == 3 else 0))
    s -= 4 * AB
    if s < 4:
        jp = s
        return [("w_pw2", 8, (2 * jp) * 128, 128, 0), inproj(2, 2 * jp, 1),
                ("w_pw2", 8, (2 * jp + 1) * 128, 128, 2048), inproj(2, 2 * jp + 1, 3)]
    s -= 4
    if s < 8:
        hh, c = divmod(s, 4)
        h0 = 4 * hh
        if c == 0:
            return [inproj(4, h0, 0), inproj(3, h0, 1), inproj(4, h0 + 1, 2), inproj(3, h0 + 1, 3)]
        if c == 1:
            return [inproj(4, h0 + 2, 0), inproj(3, h0 + 2, 1), inproj(4, h0 + 3, 2), inproj(3, h0 + 3, 3)]
        if c == 2:
            return [inproj(6, h0 + i, i) for i in range(4)]
        return [("w_in", 8, 5 * 1024 + hh * 512, 512, 0)]
    s -= 8
    if s < 4:
        jp = s
        return [("w_out", 16, (2 * jp) * 128, 128, 0), ("w_out", 16, (2 * jp + 1) * 128, 128, 2048)]
    s -= 4
    t = s
    return [("w_pg", 8, (2 * t) * 128, 128, 0), ("w_pg", 8, (2 * t + 1) * 128, 128, 1024),
            ("w_pp", 2, (2 * t) * 128, 128, 2048), ("w_pp", 2, (2 * t + 1) * 128, 128, 2304)]


S_A = 0
S_B = 4 * AB
S_C = S_B + 4
S_D = S_C + 8
S_E = S_D + 4


def build_nc(NT=SEQ // TT, debug=False):
    ntok = NT * TT
    nc = bass.Bass("TRN2", target_bir_lowering=False)
    dram = {}
    def din(name, shape):
        dram[name] = nc.dram_tensor(name, shape, F32, kind="ExternalInput").ap()
    din("x", [ntok, D])
    din("p", [ntok, 256])
    din("w_in", [D, 7 * D])
    din("w_pw2", [D, D])
    din("w_out", [2 * D, D])
    din("w_pg", [D, D])
    din("w_pp", [256, D])
    din("fpar", [128, NPAR])
    din("ln_g", [128, D])
    din("consts", [128, NCONST])
    out_d = nc.dram_tensor("out", [ntok, D], F32, kind="ExternalOutput").ap()
    wsc = nc.dram_tensor("wsc", [NSLOT, 128, SLOTW], BF16).ap()

    P = Prog()
    es = ExitStack()
    with es:
        def sb(name, shape, dt):
            return es.enter_context(nc.sbuf_tensor("sb_" + name, shape, dt))
        for e in ("pe", "act", "dve", "pool"):
            P.sems[e] = es.enter_context(nc.semaphore("s_" + e))
        dma_names = (["ring%d" % r for r in range(NR)] + ["xa0", "xa1", "xr0", "xr1", "pt", "ob0", "ob1",
                                                            "setup", "stg0", "stg1", "stg2", "stg3", "wst0", "wst1", "wst2", "wst3"])
        for n in dma_names:
            P.sems[n] = es.enter_context(nc.semaphore("d_" + n))

        def ACT(fn, reads, writes, n=TT, tbl=None, after=None):
            nd = P.op("act", fn, reads, writes, cost=(224.0 + n) / 1.2, tbl=tbl)
            if after is not None:
                nd.preds.add(after.idx)
            return nd
        def DVE(fn, reads, writes, n=TT, accel=1.0, psum=False):
            P.op("dve", fn, reads, writes, cost=(0.95 if psum else 0.85) * (58.0 + (62.0 if psum else 0.0) + n / accel) / 0.96 * (1.0 if accel == 1.0 else 1.3))
        def POOL(fn, reads, writes, n=TT):
            P.op("pool", fn, reads, writes, cost=300.0 + 5.0 * n)
        def PE(fns, reads, writes, cols):
            P.group("pe", fns, reads, writes, cost=1.15 * (cols / 2.2 + 12.0 * len(fns)))
        SILU, LNEXP = "silu", "lnexp"

        fpar = sb("fpar", [128, NPAR], F32)
        dpar = sb("dpar", [128, NDPAR], F32)
        cst = sb("cst", [128, NCONST], F32)
        lng = sb("lng", [128, D], F32)
        ident_bf = sb("ident_bf", [128, 128], BF16)
        cmat_bf = sb("cmat_bf", [128, 128], BF16)
        onesm_bf = sb("onesm_bf", [128, 128], BF16)
        ones1k_bf = sb("ones1k_bf", [128, 128], BF16)
        amask_bf = sb("amask_bf", [128, 128], BF16)
        S = sb("S", [128, 8, 128], F32)
        S_bf = sb("S_bf", [128, 8, 128], BF16)
        vbuf = sb("vbuf", [128, NU, HIST + TT], BF16)
        ring = sb("ring", [128, NR, SLOTW], BF16)
        ps = es.enter_context(nc.psum_tensor("ps", [128, 8, 512], F32))

        identf = cst[:, K_ID:K_ID + 128]
        rmask = cst[:, K_RM:K_RM + 512]
        eps_ap = dpar[:, D_EPS:D_EPS + 1]

        def col(t, c):
            return t[:, c:c + 1]

        P.dma("sp", "setup", lambda e: e.dma_start(out=fpar[:], in_=dram["fpar"]), writes=["fpar"], nbytes=128 * NPAR * 4)
        P.dma("sp", "setup", lambda e: e.dma_start(out=cst[:], in_=dram["consts"]), writes=["cst"], nbytes=128 * NCONST * 4)
        P.dma("sp", "setup", lambda e: e.dma_start(out=lng[:], in_=dram["ln_g"]), writes=["lng"], nbytes=128 * D * 4)
        P.barrier(lambda e: e.memset(dpar[:, D_EPS + 1:D_EPS + 2], 0.0))
        DVE(lambda e: e.tensor_scalar(out=dpar[:, D_CW:D_CW + 248], in0=fpar[:, C_CW:C_CW + 248],
                                      scalar1=0.5, scalar2=None, op0=ALU.mult), ["fpar"], ["dp_cw"], n=248)
        DVE(lambda e: e.tensor_tensor(out=dpar[:, D_TMP:D_TMP + 8], in0=fpar[:, C_LB0:C_LB0 + 8],
                                      in1=fpar[:, C_LB1:C_LB1 + 8], op=ALU.subtract), ["fpar"], ["dp_tmp"], n=8)
        ACT(lambda e: e.activation(out=dpar[:, D_TL:D_TL + 8], in_=dpar[:, D_TMP:D_TMP + 8],
                                   func=AF.Tanh, scale=0.5), ["dp_tmp"], ["dp_tl"], n=8, tbl=SILU)
        DVE(lambda e: e.tensor_scalar(out=dpar[:, D_A1:D_A1 + 8], in0=dpar[:, D_TL:D_TL + 8],
                                      scalar1=-0.25, scalar2=0.25, op0=ALU.mult, op1=ALU.add), ["dp_tl"], ["dp_a1"], n=8)
        DVE(lambda e: e.tensor_scalar(out=dpar[:, D_A0:D_A0 + 8], in0=dpar[:, D_TL:D_TL + 8],
                                      scalar1=0.25, scalar2=0.75, op0=ALU.mult, op1=ALU.add), ["dp_tl"], ["dp_a0"], n=8)
        DVE(lambda e: e.tensor_scalar(out=dpar[:, D_NA1:D_NA1 + 8], in0=dpar[:, D_A1:D_A1 + 8],
                                      scalar1=-1.0, scalar2=None, op0=ALU.mult), ["dp_a1"], ["dp_na1"], n=8)
        DVE(lambda e: e.memset(dpar[:, D_EPS:D_EPS + 1], EPS), [], ["dp_eps"], n=1)
        for dst, c0 in ((ident_bf, K_ID), (cmat_bf, K_CM), (onesm_bf, K_ON), (ones1k_bf, K_O1K), (amask_bf, K_AM)):
            DVE(lambda e, dst=dst, c0=c0: e.tensor_copy(out=dst[:], in_=cst[:, c0:c0 + 128]), ["cst"], ["cbf%d" % c0], n=128)
        PE([lambda e: e.matmul(ps[:, 0, 0:8], lhsT=cst[:, K_CM:K_CM + 128], rhs=fpar[:, C_CB:C_CB + 8], start=True, stop=True)],
           ["cst", "fpar"], [("ps", 0)], cols=64)
        ACT(lambda e: e.activation(out=dpar[:, D_CBC:D_CBC + 8], in_=ps[:, 0, 0:8], func=AF.Copy), [("ps", 0)], ["dp_cbc"], n=8)
        DVE(lambda e: e.memset(S[:], 0.0), [], ["S"], n=1024)
        DVE(lambda e: e.memset(S_bf[:], 0.0), [], ["S_bf"], n=1024)
        DVE(lambda e: e.memset(vbuf[:], 0.0), [], ["vb%d" % j for j in range(NU)], n=NU * (HIST + TT))

        with nc.sbuf_tensor("sb_stg32", [128, 4, SLOTW], F32) as stg32, nc.sbuf_tensor("sb_stg16", [128, 4, SLOTW], BF16) as stg16:
            cast_engs = ("act", "dve")
            for r in range(4):
                DVE(lambda e, r=r: e.memset(stg32[:, r, :], 0.0), [], [("stg32", r, q) for q in range(4)], n=SLOTW, accel=2.0)
                DVE(lambda e, r=r: e.memset(stg16[:, r, :], 0.0), [], [("stg16", r, m) for m in range(16)], n=SLOTW, accel=4.0)
            for s in range(NSLOT):
                r = s % 4
                pieces = slot_pieces(s)
                k16 = [("stg16", r, m) for m in range(16)]
                if pieces[0] in ("conv", "convs"):
                    if pieces[0] == "conv":
                        taps = [(pieces[1], m) for m in range(16)]
                    else:
                        taps = [(2 * pieces[1], 16 + m) for m in range(NSH)] + [(2 * pieces[1] + 1, 16 + m) for m in range(NSH)]
                    for m, (u, k) in enumerate(taps):
                        if m % 2 == 0:
                            DVE(lambda e, r=r, m=m, u=u, k=k: e.tensor_scalar(out=stg16[:, r, m * 128:(m + 1) * 128], in0=cst[:, K_CM:K_CM + 128],
                                                                              scalar1=col(dpar, D_CW + u * CK + k), scalar2=None, op0=ALU.mult),
                                ["cst", "dp_cw"], [("stg16", r, m)], n=128, accel=2.0)
                        else:
                            ACT(lambda e, r=r, m=m, u=u, k=k: e.activation(out=stg16[:, r, m * 128:(m + 1) * 128], in_=cst[:, K_CM:K_CM + 128],
                                                                           func=AF.Identity, scale=col(dpar, D_CW + u * CK + k)),
                                ["cst", "dp_cw"], [("stg16", r, m)], n=128)
                    P.dma("pool", "wst%d" % r, lambda e, s=s, r=r: e.dma_start(out=wsc[s], in_=stg16[:, r, :]),
                          reads=k16, writes=[("wsc", s)], nbytes=128 * SLOTW * 2)
                    continue
                rk = []
                for q, (tn, kc, c0, ncol, off) in enumerate(pieces):
                    key = ("stg32", r, q)
                    rk.append(key)
                    src = dram[tn][:, c0:c0 + ncol].rearrange("(k p) c -> p k c", p=128)
                    dst = stg32[:, r, off:off + kc * ncol].rearrange("p (k c) -> p k c", c=ncol)
                    P.dma("sp", "stg%d" % r, lambda e, src=src, dst=dst: e.dma_start(out=dst, in_=src), writes=[key],
                          nbytes=128 * kc * ncol * 4)
                ce = cast_engs[s % 2]
                if ce == "act":
                    ACT(lambda e, r=r: e.activation(out=stg16[:, r, :], in_=stg32[:, r, :], func=AF.Copy), rk, k16, n=SLOTW)
                elif ce == "dve":
                    DVE(lambda e, r=r: e.tensor_copy(out=stg16[:, r, :], in_=stg32[:, r, :]), rk, k16, n=SLOTW, accel=2.0)
                else:
                    POOL(lambda e, r=r: e.tensor_copy(out=stg16[:, r, :], in_=stg32[:, r, :]), rk, k16, n=SLOTW)
                P.dma("pool", "wst%d" % r, lambda e, s=s, r=r: e.dma_start(out=wsc[s], in_=stg16[:, r, :]),
                      reads=k16, writes=[("wsc", s)], nbytes=128 * SLOTW * 2)
        P.barrier(lambda e: e.memset(dpar[:, D_EPS + 2:D_EPS + 3], 0.0))

        xa = sb("xa", [128, 2, D], F32)
        xr = sb("xr", [128, 2, NTB, 128], F32)
        ub = sb("ub", [128, 2, D], BF16)
        ssq = sb("ssq", [128, 8], F32)
        uT = sb("uT", [128, NU, TT], BF16)
        sigb = sb("sigb", [128, 2, TT], F32)
        ycv = sb("ycv", [128, 2, TT], F32)
        ybf = sb("ybf", [128, 2, TT], BF16)
        sqb = sb("sqb", [128, 2, TT], BF16)
        rst = sb("rst", [128, 2, TT], F32)
        yn = sb("yn", [128, 2, TT], F32)
        hn = sb("hn", [128, NU, TT], BF16)
        hTb = sb("hTb", [128, NU, TT], BF16)
        sgt = sb("sgt", [128, 1, TT], F32)
        sqbL = sb("sqbL", [128, 2, TT], BF16)
        rstL = sb("rstL", [128, 2, TT], F32)
        ynL = sb("ynL", [128, 1, TT], F32)
        rstO = sb("rstO", [128, 2, TT], F32)
        sigbL = sb("sigbL", [128, 1, TT], F32)
        ycvL = sb("ycvL", [128, 1, TT], F32)
        yT = sb("yT", [128, 2 * NU, TT], BF16)
        thf = sb("thf", [128, TT], F32)
        lfb = sb("lfb", [128, TT], F32)
        kkb = sb("kkb", [128, TT], F32)
        bbb = sb("bbb", [128, TT], F32)
        ebb = sb("ebb", [128, TT], F32)
        qT = sb("qT", [128, 4, TT], BF16)
        kT = sb("kT", [128, 4, TT], BF16)
        kdT = sb("kdT", [128, 1, TT], BF16)
        kdec = sb("kdec", [128, NTB, 4, 128], BF16)
        vtok = sb("vtok", [128, NTB, 512], BF16)
        sg = sb("sg", [128, 4, TT], BF16)
        eblast = sb("eblast", [128, 4, 8], F32)
        Am = sb("Am", [128, 1, 4, 128], BF16)
        hT = sb("hT", [128, NU, TT], F32)
        ob = sb("ob", [128, 1, NTB, 128], F32)
        pbf = sb("pbf", [128, NTB, 256], BF16)
        pT = sb("pT", [128, 2, TT], BF16)

        pt_v = xr[:].rearrange("p a b c -> p (a b c)").rearrange("p (b d) -> p b d", d=256)
        psbf = [ps[:, b, :].bitcast(BF16) for b in range(8)]
        import os as _os
        pools = {"E": [int(c) for c in _os.environ.get("BANKS_E", "0123")], "L": [int(c) for c in _os.environ.get("BANKS_L", "45")]}
        bank_rot = {"E": 0, "L": 0}
        cur_pool = ["E"]
        def newbank():
            pl = cur_pool[0]
            b = pools[pl][bank_rot[pl] % len(pools[pl])]
            bank_rot[pl] += 1
            return b
        B_ACC = 6
        B_MSQ = 7

        slot_ctr = [0]
        act_fence = [None]
        ring_owner = {}

        class RingSlot(int):
            def __new__(cls, r, n):
                o = int.__new__(cls, r)
                o.n = n
                return o

        def chk(r):
            assert ring_owner[int(r)] == r.n, "ring slot reused before its last consumer was emitted"
            return int(r)
        def load_slot(tile, s):
            n = slot_ctr[0]
            slot_ctr[0] += 1
            r = n % NR
            ring_owner[r] = n
            P.dma("sp", "ring%d" % r, lambda e: e.dma_start(out=ring[:, r, :], in_=wsc[s]),
                  reads=[("wsc", s)], writes=[("ring", r)], nbytes=128 * SLOTW * 2)
            return RingSlot(r, n)

        def mm_unit(bank, r, off, nk, rhs_fn, rkeys, extra=None, extra_cols=0):
            r = chk(r)
            fns = []
            total = nk + (len(extra) if extra else 0)
            for k in range(nk):
                fns.append(lambda e, k=k: e.matmul(ps[:, bank, :], lhsT=ring[:, r, off + k * 128: off + (k + 1) * 128],
                                                   rhs=rhs_fn(k), start=(k == 0), stop=(k == total - 1)))
            if extra:
                fns.extend(extra)
            PE(fns, [("ring", r)] + rkeys, [("ps", bank)], cols=nk * TT + extra_cols)

        def rsqrt_bank(bank, dst, dkey):
            ACT(lambda e: e.activation(out=dst, in_=ps[:, bank, :], func=AF.Ln, bias=eps_ap), [("ps", bank), "dp_eps"], [dkey], tbl=LNEXP)
            ACT(lambda e: e.activation(out=dst, in_=dst, func=AF.Exp, scale=-0.5), [dkey], [dkey], tbl=LNEXP)

        uT_keys = [("uT", tb) for tb in range(NTB)]

        def load_xa(tile, tb):
            r2 = tb % 2
            t0 = tile * TT
            P.dma("sp", "xa%d" % r2, lambda e: e.dma_start(out=xa[:, r2, :], in_=dram["x"][t0 + tb * 128:t0 + (tb + 1) * 128, :]),
                  writes=[("xa", r2)], nbytes=128 * D * 4)

        def load_p(tile):
            t0 = tile * TT
            P.dma("sp", "pt", lambda e: e.dma_start(out=pt_v, in_=dram["p"][t0:t0 + TT, :].rearrange("(b p) d -> p b d", p=128)),
                  writes=["pt", ("xr", 0), ("xr", 1)], nbytes=TT * 256 * 4)

        def prefetch_p(tile):
            load_p(tile)
            DVE(lambda e: e.tensor_copy(out=pbf[:], in_=pt_v), ["pt", ("xr", 0), ("xr", 1)], ["pbf"], n=1024, accel=2.0)

        def prefetch_x(tile):
            load_xa(tile, 0)
            load_xa(tile, 1)

        def p_chain():
            cur_pool[0] = "E"
            bk = newbank()
            PE([lambda e, tb=tb, k2=k2, bk=bk: e.transpose(out=psbf[bk][:, k2 * 512 + tb * 128: k2 * 512 + (tb + 1) * 128],
                                                           in_=pbf[:, tb, k2 * 128:(k2 + 1) * 128], identity=ident_bf[:])
                for tb in range(NTB) for k2 in range(2)], ["pbf", "cbf%d" % K_ID], [("ps", bk)], cols=1024)
            ACT(lambda e, bk=bk: e.activation(out=pT[:], in_=psbf[bk].rearrange("p (k t) -> p k t", t=512), func=AF.Copy), [("ps", bk)], ["pT"], n=1024)

        def stage_a(tile):
            t0 = tile * TT
            cur_pool[0] = "E"
            if tile > 0:
                ACT(lambda e: e.activation(out=vbuf[:, :, 0:HIST], in_=vbuf[:, :, TT:TT + HIST], func=AF.Copy),
                    ["vb%d" % j for j in range(NU)], ["vh"], n=NU * HIST)
            for tb in range(NTB):
                r2 = tb % 2
                if tb >= 2:
                    load_xa(tile, tb)
                ACT(lambda e, tb=tb, r2=r2: e.activation(out=ub[:, r2, :], in_=xa[:, r2, :], func=AF.Square, accum_out=ssq[:, tb:tb + 1]),
                    [("xa", r2)], [("ub", r2), ("ssq", tb)], n=D)
                ACT(lambda e, tb=tb: e.activation(out=ssq[:, 4 + tb:5 + tb], in_=ssq[:, tb:tb + 1], func=AF.Ln, bias=eps_ap, scale=1.0 / D),
                    [("ssq", tb), "dp_eps"], [("ssq2", tb)], n=1, tbl=LNEXP)
                ACT(lambda e, tb=tb: e.activation(out=ssq[:, 4 + tb:5 + tb], in_=ssq[:, 4 + tb:5 + tb], func=AF.Exp, scale=-0.5),
                    [("ssq2", tb)], [("ssq2", tb)], n=1, tbl=LNEXP)
                DVE(lambda e, tb=tb, r2=r2: e.scalar_tensor_tensor(out=ub[:, r2, :], in0=xa[:, r2, :], scalar=ssq[:, 4 + tb:5 + tb], in1=lng[:],
                                                                   op0=ALU.mult, op1=ALU.mult),
                    [("xa", r2), ("ssq2", tb), "lng"], [("ub", r2)], n=D)
                bk = newbank()
                PE([lambda e, k=k, r2=r2, bk=bk: e.transpose(out=psbf[bk][:, k * 128:(k + 1) * 128], in_=ub[:, r2, k * 128:(k + 1) * 128], identity=ident_bf[:])
                    for k in range(NU)], [("ub", r2), "cbf%d" % K_ID], [("ps", bk)], cols=NU * 128)
                ACT(lambda e, tb=tb, bk=bk: e.activation(out=uT[:, :, tb * 128:(tb + 1) * 128], in_=psbf[bk].rearrange("p (k t) -> p k t", t=128), func=AF.Copy),
                    [("ps", bk)], [("uT", tb)], n=D)

        for tile in range(NT):
            t0 = tile * TT
            inproj_r = [None]
            shared_r = [None]

            def conv_unit(j):
                cur_pool[0] = "E"
                jp, jj = divmod(j, 2)
                if jj == 0:
                    inproj_r[0] = load_slot(tile, S_A + AB * jp)
                r = inproj_r[0]
                rot = j % 2
                bv, bg = newbank(), newbank()
                mm_unit(bv, r, (2 * jj) * 1024, 8, lambda k: uT[:, k, :], uT_keys)
                mm_unit(bg, r, (2 * jj + 1) * 1024, 8, lambda k: uT[:, k, :], uT_keys)
                ACT(lambda e, rot=rot, bg=bg: e.activation(out=sigb[:, rot, :], in_=ps[:, bg, :], func=AF.Tanh, scale=0.5),
                    [("ps", bg)], [("sigb", rot)], tbl=SILU)
                DVE(lambda e, j=j, rot=rot, bv=bv: e.scalar_tensor_tensor(out=vbuf[:, j, HIST:HIST + TT], in0=sigb[:, rot, :], scalar=1.0, in1=ps[:, bv, :],
                                                                            op0=ALU.add, op1=ALU.mult),
                    [("sigb", rot), ("ps", bv)], ["vb%d" % j], psum=True)
                yield
                rc = load_slot(tile, S_A + AB * jp + (1 if jj == 0 else 3))
                if jj == 0:
                    shared_r[0] = load_slot(tile, S_A + AB * jp + 2)
                rsh = shared_r[0]
                b1, b2 = newbank(), newbank()
                chk(rc)
                PE([lambda e, k=k, j=j, rc=int(rc), b1=b1: e.matmul(ps[:, b1, :], lhsT=ring[:, rc, k * 128:(k + 1) * 128],
                                                                   rhs=vbuf[:, j, 2 + k:2 + k + TT], start=(k == 0), stop=False) for k in range(16)],
                   [("ring", int(rc)), "vb%d" % j, "vh"], [("ps", b1)], cols=16 * TT)
                chk(rsh)
                PE([lambda e, k=k, j=j, jj=jj, rsh=int(rsh), b1=b1: e.matmul(ps[:, b1, :], lhsT=ring[:, rsh, (jj * NSH + k - 16) * 128:(jj * NSH + k - 15) * 128],
                                                                            rhs=vbuf[:, j, 2 + k:2 + k + TT], start=False, stop=False) for k in range(16, NPE_TAPS)],
                   [("ring", int(rsh)), "vb%d" % j, "vh"], [("ps", b1)], cols=NSH * TT)
                if NPE_TAPS < CK:
                    nd = CK - NPE_TAPS
                    DVE(lambda e, j=j, rot=rot: e.tensor_scalar(out=(ybf[:, rot, :] if nd == 1 else ycv[:, rot, :]),
                                                                 in0=vbuf[:, j, 2 + NPE_TAPS:2 + NPE_TAPS + TT],
                                                                 scalar1=col(dpar, D_CW + j * CK + NPE_TAPS), scalar2=None, op0=ALU.mult),
                        ["vb%d" % j, "vh", "dp_cw"], [("ybf", rot)] if nd == 1 else [("ycv", rot)], accel=2.0)
                    for k in range(NPE_TAPS + 1, CK):
                        last = (k == CK - 1)
                        outap = ybf[:, rot, :] if last else ycv[:, rot, :]
                        DVE(lambda e, j=j, rot=rot, k=k, outap=outap: e.scalar_tensor_tensor(
                            out=outap, in0=vbuf[:, j, 2 + k:2 + k + TT], scalar=col(dpar, D_CW + j * CK + k), in1=ycv[:, rot, :],
                            op0=ALU.mult, op1=ALU.add),
                            ["vb%d" % j, "vh", ("ycv", rot)], [("ybf", rot)] if last else [("ycv", rot)])
                    PE([lambda e, rot=rot, b1=b1: e.matmul(ps[:, b1, :], lhsT=cmat_bf[:], rhs=ybf[:, rot, :], start=False, stop=True)],
                       [("ybf", rot), "cbf%d" % K_CM], [("ps", b1)], cols=TT)
                yield
                cur_pool[0] = "E"
                ACT(lambda e, j=j, rot=rot, b1=b1: e.activation(out=sqb[:, rot, :], in_=ps[:, b1, :], func=AF.Square, bias=col(dpar, D_CBC + j)),
                    [("ps", b1), "dp_cbc"], [("sqb", rot)])
                PE([lambda e, rot=rot, b2=b2: e.matmul(ps[:, b2, :], lhsT=onesm_bf[:], rhs=sqb[:, rot, :], start=True, stop=True)],
                   [("sqb", rot), "cbf%d" % K_ON], [("ps", b2)], cols=TT)
                rsqrt_bank(b2, rst[:, rot, :], ("rst", rot))
                DVE(lambda e, j=j, rot=rot, b1=b1: e.scalar_tensor_tensor(out=yn[:, rot, :], in0=ps[:, b1, :], scalar=col(dpar, D_CBC + j),
                                                                            in1=rst[:, rot, :], op0=ALU.add, op1=ALU.mult),
                    [("ps", b1), ("rst", rot), "dp_cbc"], [("yn", rot)], psum=True)
                ACT(lambda e, j=j, rot=rot: e.activation(out=hn[:, j, :], in_=yn[:, rot, :], func=AF.Silu, bias=col(fpar, C_CBETA + j),
                                                         scale=col(fpar, C_CG + j)), [("yn", rot), "fpar"], [("hn", j)], tbl=SILU)

            def conv_pair(a, b):
                ga, gb = conv_unit(a), conv_unit(b)
                for _ in range(3):
                    next(ga, None)
                    next(gb, None)

            cpre_r = [None]

            def c_pre_head(hh, i):
                cur_pool[0] = "L"
                sbase = S_C + 4 * hh
                c, ii = divmod(i, 2)
                if ii == 0:
                    cpre_r[0] = load_slot(tile, sbase + c)
                r = cpre_r[0]
                if True:
                    if True:
                        h = 4 * hh + i
                        bf_, bq_ = newbank(), newbank()
                        mm_unit(bf_, r, (2 * ii) * 1024, 8, lambda k: uT[:, k, :], uT_keys)
                        mm_unit(bq_, r, (2 * ii + 1) * 1024, 8, lambda k: uT[:, k, :], uT_keys)
                        ACT(lambda e, bf_=bf_: e.activation(out=thf[:], in_=ps[:, bf_, :], func=AF.Tanh, scale=0.5), [("ps", bf_)], ["thf"], tbl=SILU, after=act_fence[0])
                        ACT(lambda e, bq_=bq_: e.activation(out=sgt[:, 0, :], in_=ps[:, bq_, :], func=AF.Silu), [("ps", bq_)], [("sgt", 0)], tbl=SILU)
                        ACT(lambda e, h=h: e.activation(out=lfb[:], in_=thf[:], func=AF.Ln, bias=col(dpar, D_A0 + h), scale=col(dpar, D_A1 + h)),
                            ["thf", "dp_a0", "dp_a1"], ["lfb"], tbl=LNEXP)
                        DVE(lambda e, h=h: e.tensor_scalar(out=kkb[:], in0=thf[:], scalar1=col(dpar, D_NA1 + h), scalar2=col(dpar, D_A1 + h),
                                                            op0=ALU.mult, op1=ALU.add), ["thf", "dp_na1", "dp_a1"], ["kkb"], accel=2.0)
                        DVE(lambda e: e.tensor_tensor_scan(out=bbb[:], data0=rmask, data1=lfb[:], initial=0.0, op0=ALU.mult, op1=ALU.add),
                            ["lfb", "cst"], ["bbb"], n=2 * TT)
                        ACT(lambda e: e.activation(out=ebb[:], in_=bbb[:], func=AF.Exp), ["bbb"], ["ebb"], tbl=LNEXP)
                        act_fence[0] = ACT(lambda e: e.activation(out=lfb[:], in_=bbb[:], func=AF.Exp, scale=-1.0), ["bbb"], ["lfb"], tbl=LNEXP)
                        DVE(lambda e, i=i: e.tensor_copy(out=eblast[:, i, :], in_=ebb[:].rearrange("p (c s) -> p c s", s=64)[:, :, 63]),
                            ["ebb"], [("eblast", i)], n=8)
                        DVE(lambda e, i=i: e.tensor_tensor(out=qT[:, i, :], in0=sgt[:, 0, :], in1=ebb[:], op=ALU.mult), [("sgt", 0), "ebb"], [("qT", i)])
                        DVE(lambda e: e.tensor_tensor(out=kkb[:], in0=kkb[:], in1=lfb[:], op=ALU.mult), ["kkb", "lfb"], ["kkb"])
                        DVE(lambda e, i=i: e.tensor_copy(out=kT[:, i, :], in_=kkb[:]), ["kkb"], [("kT", i)], accel=2.0)
                        rk = i % 2
                        DVE(lambda e, i=i, rk=rk: e.tensor_tensor(
                            out=kdT[:, 0, :].rearrange("p (c s) -> p c s", s=64), in0=kkb[:].rearrange("p (c s) -> p c s", s=64),
                            in1=eblast[:, i, :].unsqueeze(2).to_broadcast([128, 8, 64]), op=ALU.mult),
                            ["kkb", ("eblast", i)], [("kdT", 0)])
                        bk = newbank()
                        PE([lambda e, tb=tb, rk=rk, bk=bk: e.transpose(out=psbf[bk][:, tb * 128:(tb + 1) * 128], in_=kdT[:, 0, tb * 128:(tb + 1) * 128],
                                                                       identity=ident_bf[:]) for tb in range(NTB)],
                           [("kdT", 0), "cbf%d" % K_ID], [("ps", bk)], cols=NTB * 128)
                        ACT(lambda e, i=i, bk=bk: e.activation(out=kdec[:, :, i, :], in_=psbf[bk][:, 0:512].rearrange("p (b k) -> p b k", k=128), func=AF.Copy),
                            [("ps", bk)], [("kdec", i)])

            def c_pre_tail(hh):
                cur_pool[0] = "L"
                sbase = S_C + 4 * hh
                r = load_slot(tile, sbase + 2)
                for i in range(4):
                    bk = newbank()
                    mm_unit(bk, r, i * 1024, 8, lambda k: uT[:, k, :], uT_keys)
                    ACT(lambda e, i=i, bk=bk: e.activation(out=sg[:, i, :], in_=ps[:, bk, :], func=AF.Silu), [("ps", bk)], [("sg", i)], tbl=SILU)
                r = chk(load_slot(tile, sbase + 3))
                for tb in range(NTB):
                    bk = newbank()
                    PE([lambda e, k=k, tb=tb, bk=bk, r=r: e.matmul(ps[:, bk, :], lhsT=uT[:, k, tb * 128:(tb + 1) * 128], rhs=ring[:, r, k * 512:(k + 1) * 512],
                                                                   start=(k == 0), stop=(k == 7)) for k in range(8)],
                       [("ring", r)] + uT_keys, [("ps", bk)], cols=8 * TT)
                    ACT(lambda e, tb=tb, bk=bk: e.activation(out=vtok[:, tb, :], in_=ps[:, bk, :], func=AF.Copy), [("ps", bk)], [("vtok", tb)])

            qk_keys = [("qT", i) for i in range(4)] + [("kT", i) for i in range(4)]

            def rec_tb(hh, tb):
                cur_pool[0] = "L"
                ra = tb % 2
                bA = newbank()
                PE([lambda e, i=i, tb=tb, bA=bA: e.matmul(ps[:, bA, i * 128:(i + 1) * 128], lhsT=kT[:, i, tb * 128:(tb + 1) * 128],
                                                          rhs=qT[:, i, tb * 128:(tb + 1) * 128], start=(i == 0), stop=(i == 3)) for i in range(4)],
                   qk_keys, [("ps", bA)], cols=512)
                DVE(lambda e, ra=ra, bA=bA: e.tensor_tensor(out=Am[:, 0, :, :], in0=ps[:, bA, :].rearrange("p (i t) -> p i t", t=128),
                                                             in1=amask_bf[:].unsqueeze(1).to_broadcast([128, 4, 128]), op=ALU.mult),
                    [("ps", bA), "cbf%d" % K_AM], [("Am", 0)], psum=True)
                PE([lambda e, i=i, tb=tb, ra=ra: e.matmul(ps[:, B_ACC, i * 128:(i + 1) * 128], lhsT=vtok[:, tb, i * 128:(i + 1) * 128],
                                                          rhs=Am[:, 0, i, :], start=(i == 0), stop=False) for i in range(4)],
                   [("Am", 0), ("vtok", tb)], [("ps", B_ACC)], cols=512)
                for c in range(2):
                    ch = tb * 2 + c
                    pb = c * 64
                    PE([lambda e, i=i, tb=tb, c=c, hh=hh: e.matmul(
                        ps[:, B_ACC, i * 128 + c * 64: i * 128 + c * 64 + 64], lhsT=S_bf[:, 4 * hh + i, :],
                        rhs=qT[:, i, tb * 128 + c * 64: tb * 128 + c * 64 + 64], start=False, stop=(c == 1 and i == 3)) for i in range(4)],
                       ["S_bf"] + qk_keys, [("ps", B_ACC)], cols=4 * 100)
                    bS = newbank()
                    PE([lambda e, i=i, tb=tb, pb=pb, bS=bS: e.matmul(
                        ps[:, bS, i * 128:(i + 1) * 128], lhsT=kdec[pb:pb + 64, tb, i, :], rhs=vtok[pb:pb + 64, tb, i * 128:(i + 1) * 128],
                        start=(i == 0), stop=(i == 3)) for i in range(4)],
                       [("kdec", i) for i in range(4)] + [("vtok", tb)], [("ps", bS)], cols=4 * 160)
                    for i in range(4):
                        DVE(lambda e, i=i, hh=hh, ch=ch, bS=bS: e.scalar_tensor_tensor(
                            out=S[:, 4 * hh + i, :], in0=S[:, 4 * hh + i, :], scalar=eblast[:, i, ch:ch + 1],
                            in1=ps[:, bS, i * 128:(i + 1) * 128], op0=ALU.mult, op1=ALU.add),
                            [("ps", bS), ("eblast", i), ("S", hh, i)], [("S", hh, i)], n=128, psum=True)
                    DVE(lambda e, hh=hh: e.tensor_copy(out=S_bf[:, 4 * hh:4 * hh + 4, :], in_=S[:, 4 * hh:4 * hh + 4, :]),
                        [("S", hh, i) for i in range(4)], ["S_bf"], accel=2.0)
                ro = tb % 2
                ACT(lambda e, ro=ro: e.activation(out=sqbL[:, ro, :], in_=ps[:, B_ACC, :], func=AF.Square), [("ps", B_ACC)], [("sqbL", ro)])
                bm = newbank()
                PE([lambda e, ro=ro, bm=bm: e.matmul(ps[:, bm, :], lhsT=onesm_bf[:], rhs=sqbL[:, ro, :], start=True, stop=True)],
                   [("sqbL", ro), "cbf%d" % K_ON], [("ps", bm)], cols=TT)
                rsqrt_bank(bm, rstO[:, ro, :], ("rstO", ro))
                DVE(lambda e, ro=ro: e.tensor_tensor(out=ynL[:, 0, :], in0=ps[:, B_ACC, :], in1=rstO[:, ro, :], op=ALU.mult),
                    [("ps", B_ACC), ("rstO", ro)], [("ynL", 0)], psum=True)
                for i in range(4):
                    h = 4 * hh + i
                    DVE(lambda e, i=i, h=h, tb=tb, ro=ro: e.scalar_tensor_tensor(
                        out=yT[:, NU + h, tb * 128:(tb + 1) * 128], in0=ynL[:, 0, i * 128:(i + 1) * 128], scalar=col(fpar, C_ONG + h),
                        in1=sg[:, i, tb * 128:(tb + 1) * 128], op0=ALU.mult, op1=ALU.mult),
                        [("ynL", 0), ("sg", i), "fpar"], [("yT", NU + h)], n=128)

            hn_keys = [("hn", j) for j in range(NU)]
            yT_keys = [("yT", k) for k in range(2 * NU)]
            hTb_keys = [("hTb", j) for j in range(NU)]
            bslot = [None]
            dslot = [None]

            def b_unit(j):
                cur_pool[0] = "E"
                jp, jj = divmod(j, 2)
                if jj == 0:
                    bslot[0] = load_slot(tile, S_B + jp)
                r = bslot[0]
                rot = j % 2
                b1, b2 = newbank(), newbank()
                mm_unit(b1, r, (2 * jj) * 1024, 8, lambda k: hn[:, k, :], hn_keys)
                mm_unit(b2, r, (2 * jj + 1) * 1024, 8, lambda k: uT[:, k, :], uT_keys)
                ACT(lambda e, rot=rot, b2=b2: e.activation(out=sgt[:, 0, :], in_=ps[:, b2, :], func=AF.Silu), [("ps", b2)], [("sgt", 0)], tbl=SILU)
                DVE(lambda e, j=j, rot=rot, b1=b1: e.scalar_tensor_tensor(out=yT[:, j, :], in0=ps[:, b1, :], scalar=col(fpar, C_BPW2 + j),
                                                                            in1=sgt[:, 0, :], op0=ALU.add, op1=ALU.mult),
                    [("ps", b1), ("sgt", 0), "fpar"], [("yT", j)], psum=True)

            def d_unit(tile, j):
                cur_pool[0] = "E"
                t0 = tile * TT
                jp, jj = divmod(j, 2)
                if jj == 0:
                    dslot[0] = load_slot(tile, S_D + jp)
                r = dslot[0]
                rot = j % 2
                P.dma("pool", "xr%d" % rot, lambda e, t0=t0, j=j, rot=rot: e.dma_start(
                    out=xr[:, rot, :, :], in_=dram["x"][t0:t0 + TT, j * 128:(j + 1) * 128].rearrange("(b p) d -> p b d", p=128)),
                    writes=[("xr", rot)], nbytes=TT * 128 * 4 * 2)
                bh = newbank()
                extra = [lambda e, tb=tb, rot=rot, bh=bh: e.matmul(ps[:, bh, tb * 128:(tb + 1) * 128], lhsT=xr[:, rot, tb, :], rhs=identf,
                                                                  start=False, stop=(tb == NTB - 1)) for tb in range(NTB)]
                mm_unit(bh, r, jj * 2048, 16, lambda k: yT[:, k, :], yT_keys + [("xr", rot), "cst"], extra=extra, extra_cols=4 * TT)
                ACT(lambda e, j=j, bh=bh: e.activation(out=hT[:, j, :], in_=ps[:, bh, :], func=AF.Copy), [("ps", bh)], [("hT", j)])
                ACT(lambda e, rot=rot, bh=bh: e.activation(out=sqbL[:, rot, :], in_=ps[:, bh, :], func=AF.Square), [("ps", bh)], [("sqbL", rot)])
                ACT(lambda e, j=j, bh=bh: e.activation(out=hTb[:, j, :], in_=ps[:, bh, :], func=AF.Identity, scale=col(fpar, C_PEG + j)),
                    [("ps", bh), "fpar"], [("hTb", j)])
                PE([lambda e, rot=rot, j=j: e.matmul(ps[:, B_MSQ, :], lhsT=ones1k_bf[:], rhs=sqbL[:, rot, :], start=(j == 0), stop=(j == NU - 1))],
                   [("sqbL", rot), "cbf%d" % K_O1K], [("ps", B_MSQ)], cols=TT)
                if j == NU - 1:
                    rsqrt_bank(B_MSQ, rstL[:, 0, :], ("rstL", 0))

            eslots = [None]

            def e_unit(tile, j):
                cur_pool[0] = "E"
                if j % 2 == 0:
                    eslots[0] = load_slot(tile, S_E + j // 2)
                r = eslots[0]
                rot = j % 2
                bg_, bp_ = newbank(), newbank()
                mm_unit(bg_, r, (j % 2) * 1024, 8, lambda k: hTb[:, k, :], hTb_keys)
                mm_unit(bp_, r, 2048 + (j % 2) * 256, 2, lambda k: pT[:, k, :], ["pT"])
                DVE(lambda e, rot=rot, bg_=bg_: e.tensor_tensor(out=sigbL[:, 0, :], in0=ps[:, bg_, :], in1=rstL[:, 0, :], op=ALU.mult),
                    [("ps", bg_), ("rstL", 0)], [("sigbL", 0)], psum=True)
                ACT(lambda e, rot=rot: e.activation(out=sigbL[:, 0, :], in_=sigbL[:, 0, :], func=AF.Tanh, scale=0.5), [("sigbL", 0)], [("sigbL", 0)], tbl=SILU)
                DVE(lambda e, rot=rot, bp_=bp_: e.scalar_tensor_tensor(out=sigbL[:, 0, :], in0=sigbL[:, 0, :], scalar=1.0, in1=ps[:, bp_, :],
                                                                        op0=ALU.add, op1=ALU.mult), [("sigbL", 0), ("ps", bp_)], [("sigbL", 0)], psum=True)
                DVE(lambda e, rot=rot, j=j: e.scalar_tensor_tensor(out=hT[:, j, :], in0=sigbL[:, 0, :], scalar=0.5, in1=hT[:, j, :],
                                                                    op0=ALU.mult, op1=ALU.add), [("sigbL", 0), ("hT", j)], [("hT", j)])
                ACT(lambda e, rot=rot, j=j: e.activation(out=sqbL[:, rot, :], in_=hT[:, j, :], func=AF.Square), [("hT", j)], [("sqbL", rot)])
                PE([lambda e, rot=rot, j=j: e.matmul(ps[:, B_MSQ, :], lhsT=ones1k_bf[:], rhs=sqbL[:, rot, :], start=(j == 0), stop=(j == NU - 1))],
                   [("sqbL", rot), "cbf%d" % K_O1K], [("ps", B_MSQ)], cols=TT)
                if j == NU - 1:
                    rsqrt_bank(B_MSQ, rstL[:, 1, :], ("rstL", 1))

            def e_out(tile, j):
                cur_pool[0] = "E"
                t0 = tile * TT
                rot = j % 2
                DVE(lambda e, rot=rot, j=j: e.scalar_tensor_tensor(out=ycvL[:, 0, :], in0=hT[:, j, :], scalar=col(fpar, C_FG + j), in1=rstL[:, 1, :],
                                                                    op0=ALU.mult, op1=ALU.mult), [("hT", j), ("rstL", 1), "fpar"], [("ycvL", 0)])
                bk = B_MSQ
                PE([lambda e, tb=tb, rot=rot, bk=bk: e.transpose(out=ps[:, bk, tb * 128:(tb + 1) * 128], in_=ycvL[:, 0, tb * 128:(tb + 1) * 128], identity=identf)
                    for tb in range(NTB)], [("ycvL", 0), "cst"], [("ps", bk)], cols=4 * TT)
                ACT(lambda e, rot=rot, bk=bk: e.activation(out=ob[:, 0, :, :], in_=ps[:, bk, :].rearrange("p (b d) -> p b d", d=128), func=AF.Copy),
                    [("ps", bk)], [("ob", 0)])
                P.dma("pool", "ob0", lambda e, rot=rot, j=j, t0=t0: e.dma_start(
                    out=out_d[t0:t0 + TT, j * 128:(j + 1) * 128].rearrange("(b p) d -> p b d", p=128), in_=ob[:, 0, :, :]),
                    reads=[("ob", 0)], writes=[("out", tile, j)], nbytes=TT * 128 * 4 * 2)

            if tile == 0:
                prefetch_p(0)
                prefetch_x(0)
                stage_a(0)
                p_chain()
                for jp_ in range(4):
                    conv_pair(2 * jp_, 2 * jp_ + 1)
            for t in range(4):
                c_pre_head(0, t)
                if tile > 0:
                    d_unit(tile - 1, 2 * t)
                    d_unit(tile - 1, 2 * t + 1)
            c_pre_tail(0)
            for t in range(4):
                rec_tb(0, t)
                b_unit(2 * t)
                b_unit(2 * t + 1)
            if tile + 1 < NT:
                prefetch_x(tile + 1)
            for t in range(4):
                c_pre_head(1, t)
                if tile > 0:
                    e_unit(tile - 1, 2 * t)
                    e_unit(tile - 1, 2 * t + 1)
            c_pre_tail(1)
            if tile + 1 < NT:
                stage_a(tile + 1)
            for t in range(4):
                rec_tb(1, t)
                if tile > 0:
                    e_out(tile - 1, 2 * t)
                    e_out(tile - 1, 2 * t + 1)
                if tile + 1 < NT:
                    conv_pair(2 * t, 2 * t + 1)
            if tile > 0:
                p_chain()
            if tile + 1 < NT:
                prefetch_p(tile + 1)
            if tile == NT - 1:
                for j in range(NU):
                    d_unit(tile, j)
                for j in range(NU):
                    e_unit(tile, j)
                for j in range(NU):
                    e_out(tile, j)

        P.schedule()
        build_nc.last_prog = P
        with nc.Block() as block:
            @block.sync
            def _(eng):
                P.replay("sp", eng)

            @block.scalar
            def _(eng):
                P.replay("act", eng)

            @block.vector
            def _(eng):
                P.replay("dve", eng)

            @block.gpsimd
            def _(eng):
                P.replay("pool", eng)

            @block.tensor
            def _(eng):
                P.replay("pe", eng)
    return nc


def host_consts():
    c = np.zeros((128, NCONST), np.float32)
    eye = np.eye(128, dtype=np.float32)
    c[:, K_ID:K_ID + 128] = eye
    c[:, K_CM:K_CM + 128] = eye - 1.0 / 128.0
    c[:, K_ON:K_ON + 128] = 1.0 / 128.0
    c[:, K_O1K:K_O1K + 128] = 1.0 / 1024.0
    s = np.arange(128)[:, None]
    t = np.arange(128)[None, :]
    c[:, K_AM:K_AM + 128] = ((s // 64 == t // 64) & (t >= s)).astype(np.float32)
    rm = np.ones((128, 512), np.float32)
    rm[:, ::64] = 0.0
    c[:, K_RM:K_RM + 512] = rm
    return c


def host_fpar(conv_w, conv_b, cnorm_g, cnorm_b, b_pw2, lb_logits, onorm_g, pe_norm_g, final_g):
    f = np.zeros((128, NPAR), np.float32)
    cw = np.asarray(conv_w, np.float32)[0]
    f[:, C_CW:C_CW + 248] = cw.T.reshape(NU, 128, CK).transpose(1, 0, 2).reshape(128, NU * CK)
    def fm(v):
        return np.asarray(v, np.float32).reshape(NU, 128).T
    f[:, C_CB:C_CB + 8] = fm(conv_b[0])
    f[:, C_CG:C_CG + 8] = fm(cnorm_g[0])
    f[:, C_CBETA:C_CBETA + 8] = fm(cnorm_b[0])
    f[:, C_BPW2:C_BPW2 + 8] = fm(b_pw2[0])
    f[:, C_LB0:C_LB0 + 8] = fm(lb_logits[0])
    f[:, C_LB1:C_LB1 + 8] = fm(lb_logits[1])
    f[:, C_ONG:C_ONG + 8] = fm(onorm_g[0])
    f[:, C_PEG:C_PEG + 8] = fm(pe_norm_g[0])
    f[:, C_FG:C_FG + 8] = fm(final_g)
    return f


_NC_CACHE = {}


def kernel(x, p, ln_g, w_in, conv_w, conv_b, cnorm_g, cnorm_b, w_pw2, b_pw2,
           lb_logits, onorm_g, w_out, pe_norm_g, w_pg, w_pp, final_g):
    x = np.asarray(x, np.float32)
    p = np.asarray(p, np.float32)
    B, T, _ = x.shape
    NT = T // TT
    if NT not in _NC_CACHE:
        _NC_CACHE[NT] = build_nc(NT)
    nc = _NC_CACHE[NT]
    fpar = host_fpar(conv_w, conv_b, cnorm_g, cnorm_b, b_pw2, lb_logits, onorm_g, pe_norm_g, final_g)
    consts = host_consts()
    shared = {
        "w_in": np.ascontiguousarray(np.asarray(w_in, np.float32)[0]),
        "w_pw2": np.ascontiguousarray(np.asarray(w_pw2, np.float32)[0]),
        "w_out": np.ascontiguousarray(np.asarray(w_out, np.float32)[0]),
        "w_pg": np.ascontiguousarray(np.asarray(w_pg, np.float32)[0]),
        "w_pp": np.ascontiguousarray(np.asarray(w_pp, np.float32)[0]),
        "fpar": fpar,
        "ln_g": np.ascontiguousarray(np.broadcast_to(np.asarray(ln_g, np.float32)[0].reshape(1, D), (128, D))),
        "consts": consts,
    }
    in_maps = []
    for b in range(B):
        m = dict(shared)
        m["x"] = np.ascontiguousarray(x[b])
        m["p"] = np.ascontiguousarray(p[0, b])
        in_maps.append(m)
    res = run_bass_kernel_spmd(nc, in_maps, core_ids=list(range(B)))
    return np.stack([np.asarray(r["out"], np.float32) for r in res.results], axis=0)
```

```python
from contextlib import ExitStack

import numpy as np
import concourse.bass as bass
import concourse.mybir as mybir
from concourse.bass_utils import run_bass_kernel_spmd

F32 = mybir.dt.float32
BF16 = mybir.dt.bfloat16
AF = mybir.ActivationFunctionType
ALU = mybir.AluOpType

D = 1024
NU = 8
TT = 512
NTB = 4
CK = 31
HIST = 32
EPS = 1e-6
NPE_TAPS = 24
NSH = NPE_TAPS - 16
AB = 4
NSLOT = 4 * AB + 20
SLOTW = 4096
NR = 4
SEQ = 8192
NCORES = 8

C_CW = 0
C_CB = 248
C_CG = 256
C_CBETA = 264
C_BPW2 = 272
C_LB0 = 280
C_LB1 = 288
C_ONG = 296
C_PEG = 304
C_FG = 312
NPAR = 320
D_CW = 0
D_A0 = 248
D_A1 = 256
D_NA1 = 264
D_TMP = 272
D_TL = 280
D_EPS = 288
D_CBC = 296
NDPAR = 304
K_ID = 0
K_CM = 128
K_ON = 256
K_O1K = 384
K_AM = 512
K_RM = 640
NCONST = 640 + 512

ENGS = ("pe", "act", "dve", "pool", "sp")


class Node:
    __slots__ = ("idx", "eng", "fns", "cost", "kind", "sem", "tbl", "nbytes", "preds", "succs", "npred",
                 "ready", "start", "finish", "ev")


TBL_SWITCH_NS = 1300.0
DMA_LAT_NS = 2000.0
DMA_BW = 300.0


class Prog:
    def __init__(self):
        self.nodes = []
        self.lastw = {}
        self.readers = {}
        self.join = None
        self.sems = {}

    def _add(self, eng, fns, reads, writes, cost, kind, sem=None, tbl=None, nbytes=0):
        n = Node()
        n.idx = len(self.nodes)
        n.eng, n.fns, n.cost, n.kind, n.sem, n.tbl, n.nbytes = eng, fns, float(cost), kind, sem, tbl, nbytes
        preds = set()
        for k in reads:
            w = self.lastw.get(k)
            if w is not None:
                preds.add(w)
        for k in writes:
            w = self.lastw.get(k)
            if w is not None:
                preds.add(w)
            preds.update(self.readers.get(k, ()))
        if self.join is not None:
            preds.add(self.join)
        preds.discard(n.idx)
        n.preds = preds
        n.succs = []
        self.nodes.append(n)
        for k in reads:
            self.readers.setdefault(k, []).append(n.idx)
        for k in writes:
            self.lastw[k] = n.idx
            self.readers[k] = []
        return n

    def op(self, eng, fn, reads=(), writes=(), cost=600.0, tbl=None):
        return self._add(eng, [fn], reads, writes, cost, "op", tbl=tbl)

    def group(self, eng, fns, reads=(), writes=(), cost=600.0):
        return self._add(eng, list(fns), reads, writes, cost, "op")

    def dma(self, eng, semname, fn, reads=(), writes=(), nbytes=1 << 20):
        return self._add(eng, [fn], reads, writes, 100.0, "dma", sem=semname, nbytes=nbytes)

    def barrier(self, fn):
        n = self._add("dve", [fn], (), (), 100.0, "op")
        n.preds = set(range(n.idx))
        self.join = n.idx

    def schedule(self):
        import heapq
        nodes = self.nodes
        import os as _os
        W = int(_os.environ.get('SCHED_WINDOW', '0'))
        if W:
            for n in nodes:
                if n.idx >= W:
                    n.preds.add(n.idx - W)
        RE = _os.environ.get('SCHED_ENG', 'pe,act,dve')
        if RE is not None:
            allowed = set(RE.split(',')) if RE else set()
            lastn = {}
            for n in nodes:
                if n.eng not in allowed:
                    if n.eng in lastn:
                        n.preds.add(lastn[n.eng])
                    lastn[n.eng] = n.idx
        for n in nodes:
            n.npred = len(n.preds)
            n.ready = 0.0
            for p in n.preds:
                nodes[p].succs.append(n.idx)
        pending = {e: [] for e in ENGS}
        avail = {e: [] for e in ENGS}
        free = {e: 0.0 for e in ENGS}
        cur_tbl = [None]
        dma_free = [0.0]
        for n in nodes:
            if n.npred == 0:
                heapq.heappush(pending[n.eng], (0.0, n.idx))
        order = {e: [] for e in ENGS}
        done = 0
        N = len(nodes)
        import os as _os
        if _os.environ.get('SCHED_INORDER'):
            for n in nodes:
                e = n.eng
                n.start = max(free[e], n.ready)
                free[e] = n.start + n.cost
                n.finish = free[e] + (DMA_LAT_NS if n.kind == 'dma' else 0.0)
                order[e].append(n.idx)
                for s_ in n.succs:
                    m = nodes[s_]
                    if n.finish > m.ready:
                        m.ready = n.finish
            done = N
        while done < N:
            best = None
            for e in ENGS:
                pe_, av = pending[e], avail[e]
                while pe_ and pe_[0][0] <= free[e]:
                    heapq.heappush(av, heapq.heappop(pe_)[1])
                if av:
                    cand = (free[e], 0, e)
                elif pe_:
                    cand = (pe_[0][0], 1, e)
                else:
                    continue
                if best is None or cand < best:
                    best = cand
            start, which, e = best
            if which == 0:
                av = avail[e]
                if e == "act" and len(av) > 1:
                    lo = av[0]
                    pick = None
                    for i in sorted(av)[:24]:
                        t = nodes[i].tbl
                        if t is None or t == cur_tbl[0]:
                            if i - lo < 400:
                                pick = i
                            break
                    if pick is None:
                        pick = lo
                    av.remove(pick)
                    heapq.heapify(av)
                    ni = pick
                else:
                    ni = heapq.heappop(av)
            else:
                ni = heapq.heappop(pending[e])[1]
            n = nodes[ni]
            n.start = start
            if n.kind == "dma":
                free[e] = start + n.cost
                t0 = max(start, dma_free[0])
                dma_free[0] = t0 + n.nbytes / DMA_BW
                n.finish = dma_free[0] + DMA_LAT_NS
            else:
                c = n.cost
                if e == "act" and n.tbl is not None and n.tbl != cur_tbl[0]:
                    c += TBL_SWITCH_NS
                    cur_tbl[0] = n.tbl
                free[e] = start + c
                n.finish = free[e]
            order[e].append(ni)
            done += 1
            for s_ in n.succs:
                m = nodes[s_]
                if n.finish > m.ready:
                    m.ready = n.finish
                m.npred -= 1
                if m.npred == 0:
                    heapq.heappush(pending[m.eng], (m.ready, m.idx))
        self.order = order
        self.makespan = max(n.finish for n in nodes)
        self.busy = {e: sum(nodes[i].cost for i in order[e]) for e in ENGS}
        cnt = {}
        seen = {e: {} for e in ENGS}
        self.prog = {e: [] for e in ENGS}
        glob = sorted(range(N), key=lambda i: (nodes[i].start, i))
        for ni in glob:
            n = nodes[ni]
            e = n.eng
            need = {}
            for p in n.preds:
                k, v = nodes[p].ev
                if e == "pe" and k == "pe":
                    continue
                if need.get(k, 0) < v:
                    need[k] = v
            sn = seen[e]
            for k, v in need.items():
                if sn.get(k, 0) >= v:
                    continue
                sn[k] = v
                self.prog[e].append(("wait", k, v))
            if n.kind == "dma":
                c = cnt.get(n.sem, 0) + 16
                cnt[n.sem] = c
                n.ev = (n.sem, c)
                self.prog[e].append(("op", n.fns[0], n.sem, 16))
            else:
                for fn in n.fns[:-1]:
                    self.prog[e].append(("op", fn, None, 0))
                c = cnt.get(e, 0) + 1
                cnt[e] = c
                n.ev = (e, c)
                self.prog[e].append(("op", n.fns[-1], e, 1))
        for k, v in cnt.items():
            if seen["pool"].get(k, 0) < v:
                self.prog["pool"].append(("wait", k, v))
        self.final_counts = cnt

    def replay(self, eng, engobj):
        for item in self.prog[eng]:
            if item[0] == "wait":
                engobj.wait_ge(self.sems[item[1]], item[2])
            else:
                _, fn, semname, inc = item
                ins = fn(engobj)
                if semname is not None:
                    ins.then_inc(self.sems[semname], inc)


def slot_pieces(s):
    def inproj(seg, j, q):
        return ("w_in", 8, seg * 1024 + j * 128, 128, q * 1024)
    if s < 4 * AB:
        jp, t = divmod(s, AB)
        if t == 0:
            return [inproj(0, 2 * jp, 0), inproj(1, 2 * jp, 1), inproj(0, 2 * jp + 1, 2), inproj(1, 2 * jp + 1, 3)]
        if t == 2:
            return ("convs", jp)
        return ("conv", 2 * jp + (1 if t == 3 else 0))
    s -= 4 * AB
    if s < 4:
        jp = s
        return [("w_pw2", 8, (2 * jp) * 128, 128, 0), inproj(2, 2 * jp, 1),
                ("w_pw2", 8, (2 * jp + 1) * 128, 128, 2048), inproj(2, 2 * jp + 1, 3)]
    s -= 4
    if s < 8:
        hh, c = divmod(s, 4)
        h0 = 4 * hh
        if c == 0:
            return [inproj(4, h0, 0), inproj(3, h0, 1), inproj(4, h0 + 1, 2), inproj(3, h0 + 1, 3)]
        if c == 1:
            return [inproj(4, h0 + 2, 0), inproj(3, h0 + 2, 1), inproj(4, h0 + 3, 2), inproj(3, h0 + 3, 3)]
        if c == 2:
            return [inproj(6, h0 + i, i) for i in range(4)]
        return [("w_in", 8, 5 * 1024 + hh * 512, 512, 0)]
    s -= 8
    if s < 4:
        jp = s
        return [("w_out", 16, (2 * jp) * 128, 128, 0), ("w_out", 16, (2 * jp + 1) * 128, 128, 2048)]
    s -= 4
    t = s
    return [("w_pg", 8, (2 * t) * 128, 128, 0), ("w_pg", 8, (2 * t + 1) * 128, 128, 1024),
            ("w_pp", 2, (2 * t) * 128, 128, 2048), ("w_pp", 2, (2 * t + 1) * 128, 128, 2304)]


S_A = 0
S_B = 4 * AB
S_C = S_B + 4
S_D = S_C + 8
S_E = S_D + 4


def build_nc(NT=SEQ // TT, debug=False):
    ntok = NT * TT
    nc = bass.Bass("TRN2", target_bir_lowering=False)
    dram = {}
    def din(name, shape):
        dram[name] = nc.dram_tensor(name, shape, F32, kind="ExternalInput").ap()
    din("x", [ntok, D])
    din("p", [ntok, 256])
    din("w_in", [D, 7 * D])
    din("w_pw2", [D, D])
    din("w_out", [2 * D, D])
    din("w_pg", [D, D])
    din("w_pp", [256, D])
    din("fpar", [128, NPAR])
    din("ln_g", [128, D])
    din("consts", [128, NCONST])
    out_d = nc.dram_tensor("out", [ntok, D], F32, kind="ExternalOutput").ap()
    wsc = nc.dram_tensor("wsc", [NSLOT, 128, SLOTW], BF16).ap()

    P = Prog()
    es = ExitStack()
    with es:
        def sb(name, shape, dt):
            return es.enter_context(nc.sbuf_tensor("sb_" + name, shape, dt))
        for e in ("pe", "act", "dve", "pool"):
            P.sems[e] = es.enter_context(nc.semaphore("s_" + e))
        dma_names = (["ring%d" % r for r in range(NR)] + ["xa0", "xa1", "xr0", "xr1", "pt", "ob0", "ob1",
                                                            "setup", "stg0", "stg1", "stg2", "stg3", "wst0", "wst1", "wst2", "wst3"])
        for n in dma_names:
            P.sems[n] = es.enter_context(nc.semaphore("d_" + n))

        def ACT(fn, reads, writes, n=TT, tbl=None, after=None):
            nd = P.op("act", fn, reads, writes, cost=(224.0 + n) / 1.2, tbl=tbl)
            if after is not None:
                nd.preds.add(after.idx)
            return nd
        def DVE(fn, reads, writes, n=TT, accel=1.0, psum=False):
            P.op("dve", fn, reads, writes, cost=(0.95 if psum else 0.85) * (58.0 + (62.0 if psum else 0.0) + n / accel) / 0.96 * (1.0 if accel == 1.0 else 1.3))
        def POOL(fn, reads, writes, n=TT):
            P.op("pool", fn, reads, writes, cost=300.0 + 5.0 * n)
        def PE(fns, reads, writes, cols):
            P.group("pe", fns, reads, writes, cost=1.15 * (cols / 2.2 + 12.0 * len(fns)))
        SILU, LNEXP = "silu", "lnexp"

        fpar = sb("fpar", [128, NPAR], F32)
        dpar = sb("dpar", [128, NDPAR], F32)
        cst = sb("cst", [128, NCONST], F32)
        lng = sb("lng", [128, D], F32)
        ident_bf = sb("ident_bf", [128, 128], BF16)
        cmat_bf = sb("cmat_bf", [128, 128], BF16)
        onesm_bf = sb("onesm_bf", [128, 128], BF16)
        ones1k_bf = sb("ones1k_bf", [128, 128], BF16)
        amask_bf = sb("amask_bf", [128, 128], BF16)
        S = sb("S", [128, 8, 128], F32)
        S_bf = sb("S_bf", [128, 8, 128], BF16)
        vbuf = sb("vbuf", [128, NU, HIST + TT], BF16)
        ring = sb("ring", [128, NR, SLOTW], BF16)
        ps = es.enter_context(nc.psum_tensor("ps", [128, 8, 512], F32))

        identf = cst[:, K_ID:K_ID + 128]
        rmask = cst[:, K_RM:K_RM + 512]
        eps_ap = dpar[:, D_EPS:D_EPS + 1]

        def col(t, c):
            return t[:, c:c + 1]

        P.dma("sp", "setup", lambda e: e.dma_start(out=fpar[:], in_=dram["fpar"]), writes=["fpar"], nbytes=128 * NPAR * 4)
        P.dma("sp", "setup", lambda e: e.dma_start(out=cst[:], in_=dram["consts"]), writes=["cst"], nbytes=128 * NCONST * 4)
        P.dma("sp", "setup", lambda e: e.dma_start(out=lng[:], in_=dram["ln_g"]), writes=["lng"], nbytes=128 * D * 4)
        P.barrier(lambda e: e.memset(dpar[:, D_EPS + 1:D_EPS + 2], 0.0))
        DVE(lambda e: e.tensor_scalar(out=dpar[:, D_CW:D_CW + 248], in0=fpar[:, C_CW:C_CW + 248],
                                      scalar1=0.5, scalar2=None, op0=ALU.mult), ["fpar"], ["dp_cw"], n=248)
        DVE(lambda e: e.tensor_tensor(out=dpar[:, D_TMP:D_TMP + 8], in0=fpar[:, C_LB0:C_LB0 + 8],
                                      in1=fpar[:, C_LB1:C_LB1 + 8], op=ALU.subtract), ["fpar"], ["dp_tmp"], n=8)
        ACT(lambda e: e.activation(out=dpar[:, D_TL:D_TL + 8], in_=dpar[:, D_TMP:D_TMP + 8],
                                   func=AF.Tanh, scale=0.5), ["dp_tmp"], ["dp_tl"], n=8, tbl=SILU)
        DVE(lambda e: e.tensor_scalar(out=dpar[:, D_A1:D_A1 + 8], in0=dpar[:, D_TL:D_TL + 8],
                                      scalar1=-0.25, scalar2=0.25, op0=ALU.mult, op1=ALU.add), ["dp_tl"], ["dp_a1"], n=8)
        DVE(lambda e: e.tensor_scalar(out=dpar[:, D_A0:D_A0 + 8], in0=dpar[:, D_TL:D_TL + 8],
                                      scalar1=0.25, scalar2=0.75, op0=ALU.mult, op1=ALU.add), ["dp_tl"], ["dp_a0"], n=8)
        DVE(lambda e: e.tensor_scalar(out=dpar[:, D_NA1:D_NA1 + 8], in0=dpar[:, D_A1:D_A1 + 8],
                                      scalar1=-1.0, scalar2=None, op0=ALU.mult), ["dp_a1"], ["dp_na1"], n=8)
        DVE(lambda e: e.memset(dpar[:, D_EPS:D_EPS + 1], EPS), [], ["dp_eps"], n=1)
        for dst, c0 in ((ident_bf, K_ID), (cmat_bf, K_CM), (onesm_bf, K_ON), (ones1k_bf, K_O1K), (amask_bf, K_AM)):
            DVE(lambda e, dst=dst, c0=c0: e.tensor_copy(out=dst[:], in_=cst[:, c0:c0 + 128]), ["cst"], ["cbf%d" % c0], n=128)
        PE([lambda e: e.matmul(ps[:, 0, 0:8], lhsT=cst[:, K_CM:K_CM + 128], rhs=fpar[:, C_CB:C_CB + 8], start=True, stop=True)],
           ["cst", "fpar"], [("ps", 0)], cols=64)
        ACT(lambda e: e.activation(out=dpar[:, D_CBC:D_CBC + 8], in_=ps[:, 0, 0:8], func=AF.Copy), [("ps", 0)], ["dp_cbc"], n=8)
        DVE(lambda e: e.memset(S[:], 0.0), [], ["S"], n=1024)
        DVE(lambda e: e.memset(S_bf[:], 0.0), [], ["S_bf"], n=1024)
        DVE(lambda e: e.memset(vbuf[:], 0.0), [], ["vb%d" % j for j in range(NU)], n=NU * (HIST + TT))

        with nc.sbuf_tensor("sb_stg32", [128, 4, SLOTW], F32) as stg32, nc.sbuf_tensor("sb_stg16", [128, 4, SLOTW], BF16) as stg16:
            cast_engs = ("act", "dve")
            for r in range(4):
                DVE(lambda e, r=r: e.memset(stg32[:, r, :], 0.0), [], [("stg32", r, q) for q in range(4)], n=SLOTW, accel=2.0)
                DVE(lambda e, r=r: e.memset(stg16[:, r, :], 0.0), [], [("stg16", r, m) for m in range(16)], n=SLOTW, accel=4.0)
            for s in range(NSLOT):
                r = s % 4
                pieces = slot_pieces(s)
                k16 = [("stg16", r, m) for m in range(16)]
                if pieces[0] in ("conv", "convs"):
                    if pieces[0] == "conv":
                        taps = [(pieces[1], m) for m in range(16)]
                    else:
                        taps = [(2 * pieces[1], 16 + m) for m in range(NSH)] + [(2 * pieces[1] + 1, 16 + m) for m in range(NSH)]
                    for m, (u, k) in enumerate(taps):
                        if m % 2 == 0:
                            DVE(lambda e, r=r, m=m, u=u, k=k: e.tensor_scalar(out=stg16[:, r, m * 128:(m + 1) * 128], in0=cst[:, K_CM:K_CM + 128],
                                                                              scalar1=col(dpar, D_CW + u * CK + k), scalar2=None, op0=ALU.mult),
                                ["cst", "dp_cw"], [("stg16", r, m)], n=128, accel=2.0)
                        else:
                            ACT(lambda e, r=r, m=m, u=u, k=k: e.activation(out=stg16[:, r, m * 128:(m + 1) * 128], in_=cst[:, K_CM:K_CM + 128],
                                                                           func=AF.Identity, scale=col(dpar, D_CW + u * CK + k)),
                                ["cst", "dp_cw"], [("stg16", r, m)], n=128)
                    P.dma("pool", "wst%d" % r, lambda e, s=s, r=r: e.dma_start(out=wsc[s], in_=stg16[:, r, :]),
                          reads=k16, writes=[("wsc", s)], nbytes=128 * SLOTW * 2)
                    continue
                rk = []
                for q, (tn, kc, c0, ncol, off) in enumerate(pieces):
                    key = ("stg32", r, q)
                    rk.append(key)
                    src = dram[tn][:, c0:c0 + ncol].rearrange("(k p) c -> p k c", p=128)
                    dst = stg32[:, r, off:off + kc * ncol].rearrange("p (k c) -> p k c", c=ncol)
                    P.dma("sp", "stg%d" % r, lambda e, src=src, dst=dst: e.dma_start(out=dst, in_=src), writes=[key],
                          nbytes=128 * kc * ncol * 4)
                ce = cast_engs[s % 2]
                if ce == "act":
                    ACT(lambda e, r=r: e.activation(out=stg16[:, r, :], in_=stg32[:, r, :], func=AF.Copy), rk, k16, n=SLOTW)
                elif ce == "dve":
                    DVE(lambda e, r=r: e.tensor_copy(out=stg16[:, r, :], in_=stg32[:, r, :]), rk, k16, n=SLOTW, accel=2.0)
                else:
                    POOL(lambda e, r=r: e.tensor_copy(out=stg16[:, r, :], in_=stg32[:, r, :]), rk, k16, n=SLOTW)
                P.dma("pool", "wst%d" % r, lambda e, s=s, r=r: e.dma_start(out=wsc[s], in_=stg16[:, r, :]),
                      reads=k16, writes=[("wsc", s)], nbytes=128 * SLOTW * 2)
        P.barrier(lambda e: e.memset(dpar[:, D_EPS + 2:D_EPS + 3], 0.0))

        xa = sb("xa", [128, 2, D], F32)
        xr = sb("xr", [128, 2, NTB, 128], F32)
        ub = sb("ub", [128, 2, D], BF16)
        ssq = sb("ssq", [128, 8], F32)
        uT = sb("uT", [128, NU, TT], BF16)
        sigb = sb("sigb", [128, 2, TT], F32)
        ycv = sb("ycv", [128, 2, TT], F32)
        ybf = sb("ybf", [128, 2, TT], BF16)
        sqb = sb("sqb", [128, 2, TT], BF16)
        rst = sb("rst", [128, 2, TT], F32)
        yn = sb("yn", [128, 2, TT], F32)
        hn = sb("hn", [128, NU, TT], BF16)
        hTb = sb("hTb", [128, NU, TT], BF16)
        sgt = sb("sgt", [128, 1, TT], F32)
        sqbL = sb("sqbL", [128, 2, TT], BF16)
        rstL = sb("rstL", [128, 2, TT], F32)
        ynL = sb("ynL", [128, 1, TT], F32)
        rstO = sb("rstO", [128, 2, TT], F32)
        sigbL = sb("sigbL", [128, 1, TT], F32)
        ycvL = sb("ycvL", [128, 1, TT], F32)
        yT = sb("yT", [128, 2 * NU, TT], BF16)
        thf = sb("thf", [128, TT], F32)
        lfb = sb("lfb", [128, TT], F32)
        kkb = sb("kkb", [128, TT], F32)
        bbb = sb("bbb", [128, TT], F32)
        ebb = sb("ebb", [128, TT], F32)
        qT = sb("qT", [128, 4, TT], BF16)
        kT = sb("kT", [128, 4, TT], BF16)
        kdT = sb("kdT", [128, 1, TT], BF16)
        kdec = sb("kdec", [128, NTB, 4, 128], BF16)
        vtok = sb("vtok", [128, NTB, 512], BF16)
        sg = sb("sg", [128, 4, TT], BF16)
        eblast = sb("eblast", [128, 4, 8], F32)
        Am = sb("Am", [128, 1, 4, 128], BF16)
        hT = sb("hT", [128, NU, TT], F32)
        ob = sb("ob", [128, 1, NTB, 128], F32)
        pbf = sb("pbf", [128, NTB, 256], BF16)
        pT = sb("pT", [128, 2, TT], BF16)

        pt_v = xr[:].rearrange("p a b c -> p (a b c)").rearrange("p (b d) -> p b d", d=256)
        psbf = [ps[:, b, :].bitcast(BF16) for b in range(8)]
        import os as _os
        pools = {"E": [int(c) for c in _os.environ.get("BANKS_E", "0123")], "L": [int(c) for c in _os.environ.get("BANKS_L", "45")], "C": [4, 5, 6]}
        bank_rot = {"E": 0, "L": 0, "C": 0}
        cur_pool = ["E"]
        def newbank():
            pl = cur_pool[0]
            b = pools[pl][bank_rot[pl] % len(pools[pl])]
            bank_rot[pl] += 1
            return b
        B_ACC = 6
        B_MSQ = 7

        slot_ctr = [0]
        act_fence = [None]
        ring_owner = {}

        class RingSlot(int):
            def __new__(cls, r, n):
                o = int.__new__(cls, r)
                o.n = n
                return o

        def chk(r):
            assert ring_owner[int(r)] == r.n, "ring slot reused before its last consumer was emitted"
            return int(r)
        def load_slot(tile, s):
            n = slot_ctr[0]
            slot_ctr[0] += 1
            r = n % NR
            ring_owner[r] = n
            P.dma("sp", "ring%d" % r, lambda e: e.dma_start(out=ring[:, r, :], in_=wsc[s]),
                  reads=[("wsc", s)], writes=[("ring", r)], nbytes=128 * SLOTW * 2)
            return RingSlot(r, n)

        def mm_unit(bank, r, off, nk, rhs_fn, rkeys, extra=None, extra_cols=0):
            r = chk(r)
            fns = []
            total = nk + (len(extra) if extra else 0)
            for k in range(nk):
                fns.append(lambda e, k=k: e.matmul(ps[:, bank, :], lhsT=ring[:, r, off + k * 128: off + (k + 1) * 128],
                                                   rhs=rhs_fn(k), start=(k == 0), stop=(k == total - 1)))
            if extra:
                fns.extend(extra)
            PE(fns, [("ring", r)] + rkeys, [("ps", bank)], cols=nk * TT + extra_cols)

        def rsqrt_bank(bank, dst, dkey):
            ACT(lambda e: e.activation(out=dst, in_=ps[:, bank, :], func=AF.Ln, bias=eps_ap), [("ps", bank), "dp_eps"], [dkey], tbl=LNEXP)
            ACT(lambda e: e.activation(out=dst, in_=dst, func=AF.Exp, scale=-0.5), [dkey], [dkey], tbl=LNEXP)

        uT_keys = [("uT", tb) for tb in range(NTB)]

        def load_xa(tile, tb):
            r2 = tb % 2
            t0 = tile * TT
            P.dma("sp", "xa%d" % r2, lambda e: e.dma_start(out=xa[:, r2, :], in_=dram["x"][t0 + tb * 128:t0 + (tb + 1) * 128, :]),
                  writes=[("xa", r2)], nbytes=128 * D * 4)

        def load_p(tile):
            t0 = tile * TT
            P.dma("sp", "pt", lambda e: e.dma_start(out=pt_v, in_=dram["p"][t0:t0 + TT, :].rearrange("(b p) d -> p b d", p=128)),
                  writes=["pt", ("xr", 0), ("xr", 1)], nbytes=TT * 256 * 4)

        def prefetch_p(tile):
            load_p(tile)
            DVE(lambda e: e.tensor_copy(out=pbf[:], in_=pt_v), ["pt", ("xr", 0), ("xr", 1)], ["pbf"], n=1024, accel=2.0)

        def prefetch_x(tile):
            load_xa(tile, 0)
            load_xa(tile, 1)

        def p_chain():
            cur_pool[0] = "E"
            bk = newbank()
            PE([lambda e, tb=tb, k2=k2, bk=bk: e.transpose(out=psbf[bk][:, k2 * 512 + tb * 128: k2 * 512 + (tb + 1) * 128],
                                                           in_=pbf[:, tb, k2 * 128:(k2 + 1) * 128], identity=ident_bf[:])
                for tb in range(NTB) for k2 in range(2)], ["pbf", "cbf%d" % K_ID], [("ps", bk)], cols=1024)
            ACT(lambda e, bk=bk: e.activation(out=pT[:], in_=psbf[bk].rearrange("p (k t) -> p k t", t=512), func=AF.Copy), [("ps", bk)], ["pT"], n=1024)

        def stage_a(tile):
            t0 = tile * TT
            cur_pool[0] = "E"
            if tile > 0:
                ACT(lambda e: e.activation(out=vbuf[:, :, 0:HIST], in_=vbuf[:, :, TT:TT + HIST], func=AF.Copy),
                    ["vb%d" % j for j in range(NU)], ["vh"], n=NU * HIST)
            for tb in range(NTB):
                r2 = tb % 2
                if tb >= 2:
                    load_xa(tile, tb)
                ACT(lambda e, tb=tb, r2=r2: e.activation(out=ub[:, r2, :], in_=xa[:, r2, :], func=AF.Square, accum_out=ssq[:, tb:tb + 1]),
                    [("xa", r2)], [("ub", r2), ("ssq", tb)], n=D)
                ACT(lambda e, tb=tb: e.activation(out=ssq[:, 4 + tb:5 + tb], in_=ssq[:, tb:tb + 1], func=AF.Ln, bias=eps_ap, scale=1.0 / D),
                    [("ssq", tb), "dp_eps"], [("ssq2", tb)], n=1, tbl=LNEXP)
                ACT(lambda e, tb=tb: e.activation(out=ssq[:, 4 + tb:5 + tb], in_=ssq[:, 4 + tb:5 + tb], func=AF.Exp, scale=-0.5),
                    [("ssq2", tb)], [("ssq2", tb)], n=1, tbl=LNEXP)
                DVE(lambda e, tb=tb, r2=r2: e.scalar_tensor_tensor(out=ub[:, r2, :], in0=xa[:, r2, :], scalar=ssq[:, 4 + tb:5 + tb], in1=lng[:],
                                                                   op0=ALU.mult, op1=ALU.mult),
                    [("xa", r2), ("ssq2", tb), "lng"], [("ub", r2)], n=D)
                bk = newbank()
                PE([lambda e, k=k, r2=r2, bk=bk: e.transpose(out=psbf[bk][:, k * 128:(k + 1) * 128], in_=ub[:, r2, k * 128:(k + 1) * 128], identity=ident_bf[:])
                    for k in range(NU)], [("ub", r2), "cbf%d" % K_ID], [("ps", bk)], cols=NU * 128)
                ACT(lambda e, tb=tb, bk=bk: e.activation(out=uT[:, :, tb * 128:(tb + 1) * 128], in_=psbf[bk].rearrange("p (k t) -> p k t", t=128), func=AF.Copy),
                    [("ps", bk)], [("uT", tb)], n=D)

        for tile in range(NT):
            t0 = tile * TT
            inproj_r = [None]
            shared_r = [None]

            def conv_unit(j):
                cur_pool[0] = "E"
                jp, jj = divmod(j, 2)
                if jj == 0:
                    inproj_r[0] = load_slot(tile, S_A + AB * jp)
                r = inproj_r[0]
                rot = j % 2
                bv, bg = newbank(), newbank()
                mm_unit(bv, r, (2 * jj) * 1024, 8, lambda k: uT[:, k, :], uT_keys)
                mm_unit(bg, r, (2 * jj + 1) * 1024, 8, lambda k: uT[:, k, :], uT_keys)
                ACT(lambda e, rot=rot, bg=bg: e.activation(out=sigb[:, rot, :], in_=ps[:, bg, :], func=AF.Tanh, scale=0.5),
                    [("ps", bg)], [("sigb", rot)], tbl=SILU)
                DVE(lambda e, j=j, rot=rot, bv=bv: e.scalar_tensor_tensor(out=vbuf[:, j, HIST:HIST + TT], in0=sigb[:, rot, :], scalar=1.0, in1=ps[:, bv, :],
                                                                            op0=ALU.add, op1=ALU.mult),
                    [("sigb", rot), ("ps", bv)], ["vb%d" % j], psum=True)
                yield
                rc = load_slot(tile, S_A + AB * jp + (1 if jj == 0 else 3))
                if jj == 0:
                    shared_r[0] = load_slot(tile, S_A + AB * jp + 2)
                rsh = shared_r[0]
                b1, b2 = newbank(), newbank()
                chk(rc)
                PE([lambda e, k=k, j=j, rc=int(rc), b1=b1: e.matmul(ps[:, b1, :], lhsT=ring[:, rc, k * 128:(k + 1) * 128],
                                                                   rhs=vbuf[:, j, 2 + k:2 + k + TT], start=(k == 0), stop=False) for k in range(16)],
                   [("ring", int(rc)), "vb%d" % j, "vh"], [("ps", b1)], cols=16 * TT)
                chk(rsh)
                PE([lambda e, k=k, j=j, jj=jj, rsh=int(rsh), b1=b1: e.matmul(ps[:, b1, :], lhsT=ring[:, rsh, (jj * NSH + k - 16) * 128:(jj * NSH + k - 15) * 128],
                                                                            rhs=vbuf[:, j, 2 + k:2 + k + TT], start=False, stop=False) for k in range(16, NPE_TAPS)],
                   [("ring", int(rsh)), "vb%d" % j, "vh"], [("ps", b1)], cols=NSH * TT)
                if NPE_TAPS < CK:
                    nd = CK - NPE_TAPS
                    DVE(lambda e, j=j, rot=rot: e.tensor_scalar(out=(ybf[:, rot, :] if nd == 1 else ycv[:, rot, :]),
                                                                 in0=vbuf[:, j, 2 + NPE_TAPS:2 + NPE_TAPS + TT],
                                                                 scalar1=col(dpar, D_CW + j * CK + NPE_TAPS), scalar2=None, op0=ALU.mult),
                        ["vb%d" % j, "vh", "dp_cw"], [("ybf", rot)] if nd == 1 else [("ycv", rot)], accel=2.0)
                    for k in range(NPE_TAPS + 1, CK):
                        last = (k == CK - 1)
                        outap = ybf[:, rot, :] if last else ycv[:, rot, :]
                        DVE(lambda e, j=j, rot=rot, k=k, outap=outap: e.scalar_tensor_tensor(
                            out=outap, in0=vbuf[:, j, 2 + k:2 + k + TT], scalar=col(dpar, D_CW + j * CK + k), in1=ycv[:, rot, :],
                            op0=ALU.mult, op1=ALU.add),
                            ["vb%d" % j, "vh", ("ycv", rot)], [("ybf", rot)] if last else [("ycv", rot)])
                    PE([lambda e, rot=rot, b1=b1: e.matmul(ps[:, b1, :], lhsT=cmat_bf[:], rhs=ybf[:, rot, :], start=False, stop=True)],
                       [("ybf", rot), "cbf%d" % K_CM], [("ps", b1)], cols=TT)
                yield
                cur_pool[0] = "E"
                ACT(lambda e, j=j, rot=rot, b1=b1: e.activation(out=sqb[:, rot, :], in_=ps[:, b1, :], func=AF.Square, bias=col(dpar, D_CBC + j)),
                    [("ps", b1), "dp_cbc"], [("sqb", rot)])
                PE([lambda e, rot=rot, b2=b2: e.matmul(ps[:, b2, :], lhsT=onesm_bf[:], rhs=sqb[:, rot, :], start=True, stop=True)],
                   [("sqb", rot), "cbf%d" % K_ON], [("ps", b2)], cols=TT)
                rsqrt_bank(b2, rst[:, rot, :], ("rst", rot))
                DVE(lambda e, j=j, rot=rot, b1=b1: e.scalar_tensor_tensor(out=yn[:, rot, :], in0=ps[:, b1, :], scalar=col(dpar, D_CBC + j),
                                                                            in1=rst[:, rot, :], op0=ALU.add, op1=ALU.mult),
                    [("ps", b1), ("rst", rot), "dp_cbc"], [("yn", rot)], psum=True)
                ACT(lambda e, j=j, rot=rot: e.activation(out=hn[:, j, :], in_=yn[:, rot, :], func=AF.Silu, bias=col(fpar, C_CBETA + j),
                                                         scale=col(fpar, C_CG + j)), [("yn", rot), "fpar"], [("hn", j)], tbl=SILU)

            def conv_pair(a, b):
                ga, gb = conv_unit(a), conv_unit(b)
                for _ in range(3):
                    next(ga, None)
                    next(gb, None)

            cpre_r = [None]

            def c_pre_head(hh, i):
                cur_pool[0] = "C"
                sbase = S_C + 4 * hh
                c, ii = divmod(i, 2)
                if ii == 0:
                    cpre_r[0] = load_slot(tile, sbase + c)
                r = cpre_r[0]
                if True:
                    if True:
                        h = 4 * hh + i
                        bf_, bq_ = newbank(), newbank()
                        mm_unit(bf_, r, (2 * ii) * 1024, 8, lambda k: uT[:, k, :], uT_keys)
                        mm_unit(bq_, r, (2 * ii + 1) * 1024, 8, lambda k: uT[:, k, :], uT_keys)
                        ACT(lambda e, bf_=bf_: e.activation(out=thf[:], in_=ps[:, bf_, :], func=AF.Tanh, scale=0.5), [("ps", bf_)], ["thf"], tbl=SILU, after=act_fence[0])
                        ACT(lambda e, bq_=bq_: e.activation(out=sgt[:, 0, :], in_=ps[:, bq_, :], func=AF.Silu), [("ps", bq_)], [("sgt", 0)], tbl=SILU)
                        ACT(lambda e, h=h: e.activation(out=lfb[:], in_=thf[:], func=AF.Ln, bias=col(dpar, D_A0 + h), scale=col(dpar, D_A1 + h)),
                            ["thf", "dp_a0", "dp_a1"], ["lfb"], tbl=LNEXP)
                        DVE(lambda e, h=h: e.tensor_scalar(out=kkb[:], in0=thf[:], scalar1=col(dpar, D_NA1 + h), scalar2=col(dpar, D_A1 + h),
                                                            op0=ALU.mult, op1=ALU.add), ["thf", "dp_na1", "dp_a1"], ["kkb"], accel=2.0)
                        DVE(lambda e: e.tensor_tensor_scan(out=bbb[:], data0=rmask, data1=lfb[:], initial=0.0, op0=ALU.mult, op1=ALU.add),
                            ["lfb", "cst"], ["bbb"], n=2 * TT)
                        ACT(lambda e: e.activation(out=ebb[:], in_=bbb[:], func=AF.Exp), ["bbb"], ["ebb"], tbl=LNEXP)
                        act_fence[0] = ACT(lambda e: e.activation(out=lfb[:], in_=bbb[:], func=AF.Exp, scale=-1.0), ["bbb"], ["lfb"], tbl=LNEXP)
                        DVE(lambda e, i=i: e.tensor_copy(out=eblast[:, i, :], in_=ebb[:].rearrange("p (c s) -> p c s", s=64)[:, :, 63]),
                            ["ebb"], [("eblast", i)], n=8)
                        DVE(lambda e, i=i: e.tensor_tensor(out=qT[:, i, :], in0=sgt[:, 0, :], in1=ebb[:], op=ALU.mult), [("sgt", 0), "ebb"], [("qT", i)])
                        DVE(lambda e: e.tensor_tensor(out=kkb[:], in0=kkb[:], in1=lfb[:], op=ALU.mult), ["kkb", "lfb"], ["kkb"])
                        DVE(lambda e, i=i: e.tensor_copy(out=kT[:, i, :], in_=kkb[:]), ["kkb"], [("kT", i)], accel=2.0)
                        rk = i % 2
                        DVE(lambda e, i=i, rk=rk: e.tensor_tensor(
                            out=kdT[:, 0, :].rearrange("p (c s) -> p c s", s=64), in0=kkb[:].rearrange("p (c s) -> p c s", s=64),
                            in1=eblast[:, i, :].unsqueeze(2).to_broadcast([128, 8, 64]), op=ALU.mult),
                            ["kkb", ("eblast", i)], [("kdT", 0)])
                        bk = newbank()
                        PE([lambda e, tb=tb, rk=rk, bk=bk: e.transpose(out=psbf[bk][:, tb * 128:(tb + 1) * 128], in_=kdT[:, 0, tb * 128:(tb + 1) * 128],
                                                                       identity=ident_bf[:]) for tb in range(NTB)],
                           [("kdT", 0), "cbf%d" % K_ID], [("ps", bk)], cols=NTB * 128)
                        ACT(lambda e, i=i, bk=bk: e.activation(out=kdec[:, :, i, :], in_=psbf[bk][:, 0:512].rearrange("p (b k) -> p b k", k=128), func=AF.Copy),
                            [("ps", bk)], [("kdec", i)])

            def c_pre_tail(hh):
                cur_pool[0] = "C"
                sbase = S_C + 4 * hh
                r = load_slot(tile, sbase + 2)
                for i in range(4):
                    bk = newbank()
                    mm_unit(bk, r, i * 1024, 8, lambda k: uT[:, k, :], uT_keys)
                    ACT(lambda e, i=i, bk=bk: e.activation(out=sg[:, i, :], in_=ps[:, bk, :], func=AF.Silu), [("ps", bk)], [("sg", i)], tbl=SILU)
                r = chk(load_slot(tile, sbase + 3))
                for tb in range(NTB):
                    bk = newbank()
                    PE([lambda e, k=k, tb=tb, bk=bk, r=r: e.matmul(ps[:, bk, :], lhsT=uT[:, k, tb * 128:(tb + 1) * 128], rhs=ring[:, r, k * 512:(k + 1) * 512],
                                                                   start=(k == 0), stop=(k == 7)) for k in range(8)],
                       [("ring", r)] + uT_keys, [("ps", bk)], cols=8 * TT)
                    ACT(lambda e, tb=tb, bk=bk: e.activation(out=vtok[:, tb, :], in_=ps[:, bk, :], func=AF.Copy), [("ps", bk)], [("vtok", tb)])

            qk_keys = [("qT", i) for i in range(4)] + [("kT", i) for i in range(4)]

            def rec_tb(hh, tb):
                cur_pool[0] = "L"
                ra = tb % 2
                bA = newbank()
                PE([lambda e, i=i, tb=tb, bA=bA: e.matmul(ps[:, bA, i * 128:(i + 1) * 128], lhsT=kT[:, i, tb * 128:(tb + 1) * 128],
                                                          rhs=qT[:, i, tb * 128:(tb + 1) * 128], start=(i == 0), stop=(i == 3)) for i in range(4)],
                   qk_keys, [("ps", bA)], cols=512)
                DVE(lambda e, ra=ra, bA=bA: e.tensor_tensor(out=Am[:, 0, :, :], in0=ps[:, bA, :].rearrange("p (i t) -> p i t", t=128),
                                                             in1=amask_bf[:].unsqueeze(1).to_broadcast([128, 4, 128]), op=ALU.mult),
                    [("ps", bA), "cbf%d" % K_AM], [("Am", 0)], psum=True)
                PE([lambda e, i=i, tb=tb, ra=ra: e.matmul(ps[:, B_ACC, i * 128:(i + 1) * 128], lhsT=vtok[:, tb, i * 128:(i + 1) * 128],
                                                          rhs=Am[:, 0, i, :], start=(i == 0), stop=False) for i in range(4)],
                   [("Am", 0), ("vtok", tb)], [("ps", B_ACC)], cols=512)
                for c in range(2):
                    ch = tb * 2 + c
                    pb = c * 64
                    PE([lambda e, i=i, tb=tb, c=c, hh=hh: e.matmul(
                        ps[:, B_ACC, i * 128 + c * 64: i * 128 + c * 64 + 64], lhsT=S_bf[:, 4 * hh + i, :],
                        rhs=qT[:, i, tb * 128 + c * 64: tb * 128 + c * 64 + 64], start=False, stop=(c == 1 and i == 3)) for i in range(4)],
                       ["S_bf"] + qk_keys, [("ps", B_ACC)], cols=4 * 100)
                    bS = newbank()
                    PE([lambda e, i=i, tb=tb, pb=pb, bS=bS: e.matmul(
                        ps[:, bS, i * 128:(i + 1) * 128], lhsT=kdec[pb:pb + 64, tb, i, :], rhs=vtok[pb:pb + 64, tb, i * 128:(i + 1) * 128],
                        start=(i == 0), stop=(i == 3)) for i in range(4)],
                       [("kdec", i) for i in range(4)] + [("vtok", tb)], [("ps", bS)], cols=4 * 160)
                    for i in range(4):
                        DVE(lambda e, i=i, hh=hh, ch=ch, bS=bS: e.scalar_tensor_tensor(
                            out=S[:, 4 * hh + i, :], in0=S[:, 4 * hh + i, :], scalar=eblast[:, i, ch:ch + 1],
                            in1=ps[:, bS, i * 128:(i + 1) * 128], op0=ALU.mult, op1=ALU.add),
                            [("ps", bS), ("eblast", i), ("S", hh, i)], [("S", hh, i)], n=128, psum=True)
                    DVE(lambda e, hh=hh: e.tensor_copy(out=S_bf[:, 4 * hh:4 * hh + 4, :], in_=S[:, 4 * hh:4 * hh + 4, :]),
                        [("S", hh, i) for i in range(4)], ["S_bf"], accel=2.0)
                ro = tb % 2
                ACT(lambda e, ro=ro: e.activation(out=sqbL[:, ro, :], in_=ps[:, B_ACC, :], func=AF.Square), [("ps", B_ACC)], [("sqbL", ro)])
                bm = newbank()
                PE([lambda e, ro=ro, bm=bm: e.matmul(ps[:, bm, :], lhsT=onesm_bf[:], rhs=sqbL[:, ro, :], start=True, stop=True)],
                   [("sqbL", ro), "cbf%d" % K_ON], [("ps", bm)], cols=TT)
                rsqrt_bank(bm, rstO[:, ro, :], ("rstO", ro))
                DVE(lambda e, ro=ro: e.tensor_tensor(out=ynL[:, 0, :], in0=ps[:, B_ACC, :], in1=rstO[:, ro, :], op=ALU.mult),
                    [("ps", B_ACC), ("rstO", ro)], [("ynL", 0)], psum=True)
                for i in range(4):
                    h = 4 * hh + i
                    DVE(lambda e, i=i, h=h, tb=tb, ro=ro: e.scalar_tensor_tensor(
                        out=yT[:, NU + h, tb * 128:(tb + 1) * 128], in0=ynL[:, 0, i * 128:(i + 1) * 128], scalar=col(fpar, C_ONG + h),
                        in1=sg[:, i, tb * 128:(tb + 1) * 128], op0=ALU.mult, op1=ALU.mult),
                        [("ynL", 0), ("sg", i), "fpar"], [("yT", NU + h)], n=128)

            hn_keys = [("hn", j) for j in range(NU)]
            yT_keys = [("yT", k) for k in range(2 * NU)]
            hTb_keys = [("hTb", j) for j in range(NU)]
            bslot = [None]
            dslot = [None]

            def b_unit(j):
                cur_pool[0] = "E"
                jp, jj = divmod(j, 2)
                if jj == 0:
                    bslot[0] = load_slot(tile, S_B + jp)
                r = bslot[0]
                rot = j % 2
                b1, b2 = newbank(), newbank()
                mm_unit(b1, r, (2 * jj) * 1024, 8, lambda k: hn[:, k, :], hn_keys)
                mm_unit(b2, r, (2 * jj + 1) * 1024, 8, lambda k: uT[:, k, :], uT_keys)
                ACT(lambda e, rot=rot, b2=b2: e.activation(out=sgt[:, 0, :], in_=ps[:, b2, :], func=AF.Silu), [("ps", b2)], [("sgt", 0)], tbl=SILU)
                DVE(lambda e, j=j, rot=rot, b1=b1: e.scalar_tensor_tensor(out=yT[:, j, :], in0=ps[:, b1, :], scalar=col(fpar, C_BPW2 + j),
                                                                            in1=sgt[:, 0, :], op0=ALU.add, op1=ALU.mult),
                    [("ps", b1), ("sgt", 0), "fpar"], [("yT", j)], psum=True)

            def d_unit(tile, j):
                cur_pool[0] = "E"
                t0 = tile * TT
                jp, jj = divmod(j, 2)
                if jj == 0:
                    dslot[0] = load_slot(tile, S_D + jp)
                r = dslot[0]
                rot = j % 2
                P.dma("pool", "xr%d" % rot, lambda e, t0=t0, j=j, rot=rot: e.dma_start(
                    out=xr[:, rot, :, :], in_=dram["x"][t0:t0 + TT, j * 128:(j + 1) * 128].rearrange("(b p) d -> p b d", p=128)),
                    writes=[("xr", rot)], nbytes=TT * 128 * 4 * 2)
                bh = newbank()
                extra = [lambda e, tb=tb, rot=rot, bh=bh: e.matmul(ps[:, bh, tb * 128:(tb + 1) * 128], lhsT=xr[:, rot, tb, :], rhs=identf,
                                                                  start=False, stop=(tb == NTB - 1)) for tb in range(NTB)]
                mm_unit(bh, r, jj * 2048, 16, lambda k: yT[:, k, :], yT_keys + [("xr", rot), "cst"], extra=extra, extra_cols=4 * TT)
                ACT(lambda e, j=j, bh=bh: e.activation(out=hT[:, j, :], in_=ps[:, bh, :], func=AF.Copy), [("ps", bh)], [("hT", j)])
                ACT(lambda e, rot=rot, bh=bh: e.activation(out=sqbL[:, rot, :], in_=ps[:, bh, :], func=AF.Square), [("ps", bh)], [("sqbL", rot)])
                ACT(lambda e, j=j, bh=bh: e.activation(out=hTb[:, j, :], in_=ps[:, bh, :], func=AF.Identity, scale=col(fpar, C_PEG + j)),
                    [("ps", bh), "fpar"], [("hTb", j)])
                PE([lambda e, rot=rot, j=j: e.matmul(ps[:, B_MSQ, :], lhsT=ones1k_bf[:], rhs=sqbL[:, rot, :], start=(j == 0), stop=(j == NU - 1))],
                   [("sqbL", rot), "cbf%d" % K_O1K], [("ps", B_MSQ)], cols=TT)
                if j == NU - 1:
                    rsqrt_bank(B_MSQ, rstL[:, 0, :], ("rstL", 0))

            eslots = [None]

            def e_unit(tile, j):
                cur_pool[0] = "E"
                if j % 2 == 0:
                    eslots[0] = load_slot(tile, S_E + j // 2)
                r = eslots[0]
                rot = j % 2
                bg_, bp_ = newbank(), newbank()
                mm_unit(bg_, r, (j % 2) * 1024, 8, lambda k: hTb[:, k, :], hTb_keys)
                mm_unit(bp_, r, 2048 + (j % 2) * 256, 2, lambda k: pT[:, k, :], ["pT"])
                DVE(lambda e, rot=rot, bg_=bg_: e.tensor_tensor(out=sigbL[:, 0, :], in0=ps[:, bg_, :], in1=rstL[:, 0, :], op=ALU.mult),
                    [("ps", bg_), ("rstL", 0)], [("sigbL", 0)], psum=True)
                ACT(lambda e, rot=rot: e.activation(out=sigbL[:, 0, :], in_=sigbL[:, 0, :], func=AF.Tanh, scale=0.5), [("sigbL", 0)], [("sigbL", 0)], tbl=SILU)
                DVE(lambda e, rot=rot, bp_=bp_: e.scalar_tensor_tensor(out=sigbL[:, 0, :], in0=sigbL[:, 0, :], scalar=1.0, in1=ps[:, bp_, :],
                                                                        op0=ALU.add, op1=ALU.mult), [("sigbL", 0), ("ps", bp_)], [("sigbL", 0)], psum=True)
                DVE(lambda e, rot=rot, j=j: e.scalar_tensor_tensor(out=hT[:, j, :], in0=sigbL[:, 0, :], scalar=0.5, in1=hT[:, j, :],
                                                                    op0=ALU.mult, op1=ALU.add), [("sigbL", 0), ("hT", j)], [("hT", j)])
                ACT(lambda e, rot=rot, j=j: e.activation(out=sqbL[:, rot, :], in_=hT[:, j, :], func=AF.Square), [("hT", j)], [("sqbL", rot)])
                PE([lambda e, rot=rot, j=j: e.matmul(ps[:, B_MSQ, :], lhsT=ones1k_bf[:], rhs=sqbL[:, rot, :], start=(j == 0), stop=(j == NU - 1))],
                   [("sqbL", rot), "cbf%d" % K_O1K], [("ps", B_MSQ)], cols=TT)
                if j == NU - 1:
                    rsqrt_bank(B_MSQ, rstL[:, 1, :], ("rstL", 1))

            def e_out(tile, j):
                cur_pool[0] = "E"
                t0 = tile * TT
                rot = j % 2
                DVE(lambda e, rot=rot, j=j: e.scalar_tensor_tensor(out=ycvL[:, 0, :], in0=hT[:, j, :], scalar=col(fpar, C_FG + j), in1=rstL[:, 1, :],
                                                                    op0=ALU.mult, op1=ALU.mult), [("hT", j), ("rstL", 1), "fpar"], [("ycvL", 0)])
                bk = B_MSQ
                PE([lambda e, tb=tb, rot=rot, bk=bk: e.transpose(out=ps[:, bk, tb * 128:(tb + 1) * 128], in_=ycvL[:, 0, tb * 128:(tb + 1) * 128], identity=identf)
                    for tb in range(NTB)], [("ycvL", 0), "cst"], [("ps", bk)], cols=4 * TT)
                ACT(lambda e, rot=rot, bk=bk: e.activation(out=ob[:, 0, :, :], in_=ps[:, bk, :].rearrange("p (b d) -> p b d", d=128), func=AF.Copy),
                    [("ps", bk)], [("ob", 0)])
                P.dma("pool", "ob0", lambda e, rot=rot, j=j, t0=t0: e.dma_start(
                    out=out_d[t0:t0 + TT, j * 128:(j + 1) * 128].rearrange("(b p) d -> p b d", p=128), in_=ob[:, 0, :, :]),
                    reads=[("ob", 0)], writes=[("out", tile, j)], nbytes=TT * 128 * 4 * 2)

            if tile == 0:
                prefetch_p(0)
                prefetch_x(0)
                stage_a(0)
                p_chain()
                for jp_ in range(4):
                    conv_pair(2 * jp_, 2 * jp_ + 1)
            for t in range(4):
                c_pre_head(0, t)
                if tile > 0:
                    d_unit(tile - 1, 2 * t)
                    d_unit(tile - 1, 2 * t + 1)
            c_pre_tail(0)
            for t in range(4):
                rec_tb(0, t)
                b_unit(2 * t)
                b_unit(2 * t + 1)
            if tile + 1 < NT:
                prefetch_x(tile + 1)
            for t in range(4):
                c_pre_head(1, t)
                if tile > 0:
                    e_unit(tile - 1, 2 * t)
                    e_unit(tile - 1, 2 * t + 1)
            c_pre_tail(1)
            if tile + 1 < NT:
                stage_a(tile + 1)
            for t in range(4):
                rec_tb(1, t)
                if tile > 0:
                    e_out(tile - 1, 2 * t)
                    e_out(tile - 1, 2 * t + 1)
                if tile + 1 < NT:
                    conv_pair(2 * t, 2 * t + 1)
            if tile > 0:
                p_chain()
            if tile + 1 < NT:
                prefetch_p(tile + 1)
            if tile == NT - 1:
                for j in range(NU):
                    d_unit(tile, j)
                for j in range(NU):
                    e_unit(tile, j)
                for j in range(NU):
                    e_out(tile, j)

        P.schedule()
        build_nc.last_prog = P
        with nc.Block() as block:
            @block.sync
            def _(eng):
                P.replay("sp", eng)

            @block.scalar
            def _(eng):
                P.replay("act", eng)

            @block.vector
            def _(eng):
                P.replay("dve", eng)

            @block.gpsimd
            def _(eng):
                P.replay("pool", eng)

            @block.tensor
            def _(eng):
                P.replay("pe", eng)
    return nc


def host_consts():
    c = np.zeros((128, NCONST), np.float32)
    eye = np.eye(128, dtype=np.float32)
    c[:, K_ID:K_ID + 128] = eye
    c[:, K_CM:K_CM + 128] = eye - 1.0 / 128.0
    c[:, K_ON:K_ON + 128] = 1.0 / 128.0
    c[:, K_O1K:K_O1K + 128] = 1.0 / 1024.0
    s = np.arange(128)[:, None]
    t = np.arange(128)[None, :]
    c[:, K_AM:K_AM + 128] = ((s // 64 == t // 64) & (t >= s)).astype(np.float32)
    rm = np.ones((128, 512), np.float32)
    rm[:, ::64] = 0.0
    c[:, K_RM:K_RM + 512] = rm
    return c


def host_fpar(conv_w, conv_b, cnorm_g, cnorm_b, b_pw2, lb_logits, onorm_g, pe_norm_g, final_g):
    f = np.zeros((128, NPAR), np.float32)
    cw = np.asarray(conv_w, np.float32)[0]
    f[:, C_CW:C_CW + 248] = cw.T.reshape(NU, 128, CK).transpose(1, 0, 2).reshape(128, NU * CK)
    def fm(v):
        return np.asarray(v, np.float32).reshape(NU, 128).T
    f[:, C_CB:C_CB + 8] = fm(conv_b[0])
    f[:, C_CG:C_CG + 8] = fm(cnorm_g[0])
    f[:, C_CBETA:C_CBETA + 8] = fm(cnorm_b[0])
    f[:, C_BPW2:C_BPW2 + 8] = fm(b_pw2[0])
    f[:, C_LB0:C_LB0 + 8] = fm(lb_logits[0])
    f[:, C_LB1:C_LB1 + 8] = fm(lb_logits[1])
    f[:, C_ONG:C_ONG + 8] = fm(onorm_g[0])
    f[:, C_PEG:C_PEG + 8] = fm(pe_norm_g[0])
    f[:, C_FG:C_FG + 8] = fm(final_g)
    return f


_NC_CACHE = {}


def kernel(x, p, ln_g, w_in, conv_w, conv_b, cnorm_g, cnorm_b, w_pw2, b_pw2,
           lb_logits, onorm_g, w_out, pe_norm_g, w_pg, w_pp, final_g):
    x = np.asarray(x, np.float32)
    p = np.asarray(p, np.float32)
    B, T, _ = x.shape
    NT = T // TT
    if NT not in _NC_CACHE:
        _NC_CACHE[NT] = build_nc(NT)
    nc = _NC_CACHE[NT]
    fpar = host_fpar(conv_w, conv_b, cnorm_g, cnorm_b, b_pw2, lb_logits, onorm_g, pe_norm_g, final_g)
    consts = host_consts()
    shared = {
        "w_in": np.ascontiguousarray(np.asarray(w_in, np.float32)[0]),
        "w_pw2": np.ascontiguousarray(np.asarray(w_pw2, np.float32)[0]),
        "w_out": np.ascontiguousarray(np.asarray(w_out, np.float32)[0]),
        "w_pg": np.ascontiguousarray(np.asarray(w_pg, np.float32)[0]),
        "w_pp": np.ascontiguousarray(np.asarray(w_pp, np.float32)[0]),
        "fpar": fpar,
        "ln_g": np.ascontiguousarray(np.broadcast_to(np.asarray(ln_g, np.float32)[0].reshape(1, D), (128, D))),
        "consts": consts,
    }
    in_maps = []
    for b in range(B):
        m = dict(shared)
        m["x"] = np.ascontiguousarray(x[b])
        m["p"] = np.ascontiguousarray(p[0, b])
        in_maps.append(m)
    res = run_bass_kernel_spmd(nc, in_maps, core_ids=list(range(B)))
    return np.stack([np.asarray(r["out"], np.float32) for r in res.results], axis=0)
```

```python
from contextlib import ExitStack

import numpy as np
import concourse.bass as bass
import concourse.mybir as mybir
from concourse.bass_utils import run_bass_kernel_spmd

F32 = mybir.dt.float32
BF16 = mybir.dt.bfloat16
AF = mybir.ActivationFunctionType
ALU = mybir.AluOpType

D = 1024
NU = 8
TT = 512
NTB = 4
CK = 31
HIST = 32
EPS = 1e-6
NPE_TAPS = 24
NSH = NPE_TAPS - 16
AB = 4
NSLOT = 4 * AB + 20
SLOTW = 4096
NR = 4
SEQ = 8192
NCORES = 8

C_CW = 0
C_CB = 248
C_CG = 256
C_CBETA = 264
C_BPW2 = 272
C_LB0 = 280
C_LB1 = 288
C_ONG = 296
C_PEG = 304
C_FG = 312
NPAR = 320
D_CW = 0
D_A0 = 248
D_A1 = 256
D_NA1 = 264
D_TMP = 272
D_TL = 280
D_EPS = 288
D_CBC = 296
NDPAR = 304
K_ID = 0
K_CM = 128
K_ON = 256
K_O1K = 384
K_AM = 512
K_RM = 640
NCONST = 640 + 512

ENGS = ("pe", "act", "dve", "pool", "sp")


class Node:
    __slots__ = ("idx", "eng", "fns", "cost", "kind", "sem", "tbl", "nbytes", "preds", "succs", "npred",
                 "ready", "start", "finish", "ev")


TBL_SWITCH_NS = 1300.0
DMA_LAT_NS = 2000.0
DMA_BW = 300.0


class Prog:
    def __init__(self):
        self.nodes = []
        self.lastw = {}
        self.readers = {}
        self.join = None
        self.sems = {}

    def _add(self, eng, fns, reads, writes, cost, kind, sem=None, tbl=None, nbytes=0):
        n = Node()
        n.idx = len(self.nodes)
        n.eng, n.fns, n.cost, n.kind, n.sem, n.tbl, n.nbytes = eng, fns, float(cost), kind, sem, tbl, nbytes
        preds = set()
        for k in reads:
            w = self.lastw.get(k)
            if w is not None:
                preds.add(w)
        for k in writes:
            w = self.lastw.get(k)
            if w is not None:
                preds.add(w)
            preds.update(self.readers.get(k, ()))
        if self.join is not None:
            preds.add(self.join)
        preds.discard(n.idx)
        n.preds = preds
        n.succs = []
        self.nodes.append(n)
        for k in reads:
            self.readers.setdefault(k, []).append(n.idx)
        for k in writes:
            self.lastw[k] = n.idx
            self.readers[k] = []
        return n

    def op(self, eng, fn, reads=(), writes=(), cost=600.0, tbl=None):
        return self._add(eng, [fn], reads, writes, cost, "op", tbl=tbl)

    def group(self, eng, fns, reads=(), writes=(), cost=600.0):
        return self._add(eng, list(fns), reads, writes, cost, "op")

    def dma(self, eng, semname, fn, reads=(), writes=(), nbytes=1 << 20):
        return self._add(eng, [fn], reads, writes, 100.0, "dma", sem=semname, nbytes=nbytes)

    def barrier(self, fn):
        n = self._add("dve", [fn], (), (), 100.0, "op")
        n.preds = set(range(n.idx))
        self.join = n.idx

    def schedule(self):
        import heapq
        nodes = self.nodes
        import os as _os
        W = int(_os.environ.get('SCHED_WINDOW', '0'))
        if W:
            for n in nodes:
                if n.idx >= W:
                    n.preds.add(n.idx - W)
        RE = _os.environ.get('SCHED_ENG', 'pe,act,dve')
        if RE is not None:
            allowed = set(RE.split(',')) if RE else set()
            lastn = {}
            for n in nodes:
                if n.eng not in allowed:
                    if n.eng in lastn:
                        n.preds.add(lastn[n.eng])
                    lastn[n.eng] = n.idx
        for n in nodes:
            n.npred = len(n.preds)
            n.ready = 0.0
            for p in n.preds:
                nodes[p].succs.append(n.idx)
        rank = [0.0] * len(nodes)
        for n in reversed(nodes):
            r = 0.0
            for s_ in n.succs:
                if rank[s_] > r:
                    r = rank[s_]
            rank[n.idx] = r + n.cost + (DMA_LAT_NS if n.kind == 'dma' else 0.0)
        self._rank = rank
        pending = {e: [] for e in ENGS}
        avail = {e: [] for e in ENGS}
        free = {e: 0.0 for e in ENGS}
        cur_tbl = [None]
        dma_free = [0.0]
        for n in nodes:
            if n.npred == 0:
                heapq.heappush(pending[n.eng], (0.0, n.idx))
        order = {e: [] for e in ENGS}
        done = 0
        N = len(nodes)
        import os as _os
        if _os.environ.get('SCHED_INORDER'):
            for n in nodes:
                e = n.eng
                n.start = max(free[e], n.ready)
                free[e] = n.start + n.cost
                n.finish = free[e] + (DMA_LAT_NS if n.kind == 'dma' else 0.0)
                order[e].append(n.idx)
                for s_ in n.succs:
                    m = nodes[s_]
                    if n.finish > m.ready:
                        m.ready = n.finish
            done = N
        while done < N:
            best = None
            for e in ENGS:
                pe_, av = pending[e], avail[e]
                while pe_ and pe_[0][0] <= free[e]:
                    _i = heapq.heappop(pe_)[1]
                    heapq.heappush(av, (-rank[_i], _i))
                if av:
                    cand = (free[e], 0, e)
                elif pe_:
                    cand = (pe_[0][0], 1, e)
                else:
                    continue
                if best is None or cand < best:
                    best = cand
            start, which, e = best
            if which == 0:
                av = avail[e]
                if e == "act" and len(av) > 1:
                    lo = av[0]
                    pick = None
                    for ent in sorted(av)[:24]:
                        t = nodes[ent[1]].tbl
                        if t is None or t == cur_tbl[0]:
                            if ent[1] - lo[1] < 400:
                                pick = ent
                            break
                    if pick is None:
                        pick = lo
                    av.remove(pick)
                    heapq.heapify(av)
                    ni = pick[1]
                else:
                    ni = heapq.heappop(av)[1]
            else:
                ni = heapq.heappop(pending[e])[1]
            n = nodes[ni]
            n.start = start
            if n.kind == "dma":
                free[e] = start + n.cost
                t0 = max(start, dma_free[0])
                dma_free[0] = t0 + n.nbytes / DMA_BW
                n.finish = dma_free[0] + DMA_LAT_NS
            else:
                c = n.cost
                if e == "act" and n.tbl is not None and n.tbl != cur_tbl[0]:
                    c += TBL_SWITCH_NS
                    cur_tbl[0] = n.tbl
                free[e] = start + c
                n.finish = free[e]
            order[e].append(ni)
            done += 1
            for s_ in n.succs:
                m = nodes[s_]
                if n.finish > m.ready:
                    m.ready = n.finish
                m.npred -= 1
                if m.npred == 0:
                    heapq.heappush(pending[m.eng], (m.ready, m.idx))
        self.order = order
        self.makespan = max(n.finish for n in nodes)
        self.busy = {e: sum(nodes[i].cost for i in order[e]) for e in ENGS}
        cnt = {}
        seen = {e: {} for e in ENGS}
        self.prog = {e: [] for e in ENGS}
        glob = sorted(range(N), key=lambda i: (nodes[i].start, i))
        for ni in glob:
            n = nodes[ni]
            e = n.eng
            need = {}
            for p in n.preds:
                k, v = nodes[p].ev
                if e == "pe" and k == "pe":
                    continue
                if need.get(k, 0) < v:
                    need[k] = v
            sn = seen[e]
            for k, v in need.items():
                if sn.get(k, 0) >= v:
                    continue
                sn[k] = v
                self.prog[e].append(("wait", k, v))
            if n.kind == "dma":
                c = cnt.get(n.sem, 0) + 16
                cnt[n.sem] = c
                n.ev = (n.sem, c)
                self.prog[e].append(("op", n.fns[0], n.sem, 16))
            else:
                for fn in n.fns[:-1]:
                    self.prog[e].append(("op", fn, None, 0))
                c = cnt.get(e, 0) + 1
                cnt[e] = c
                n.ev = (e, c)
                self.prog[e].append(("op", n.fns[-1], e, 1))
        for k, v in cnt.items():
            if seen["pool"].get(k, 0) < v:
                self.prog["pool"].append(("wait", k, v))
        self.final_counts = cnt

    def replay(self, eng, engobj):
        for item in self.prog[eng]:
            if item[0] == "wait":
                engobj.wait_ge(self.sems[item[1]], item[2])
            else:
                _, fn, semname, inc = item
                ins = fn(engobj)
                if semname is not None:
                    ins.then_inc(self.sems[semname], inc)


def slot_pieces(s):
    def inproj(seg, j, q):
        return ("w_in", 8, seg * 1024 + j * 128, 128, q * 1024)
    if s < 4 * AB:
        jp, t = divmod(s, AB)
        if t == 0:
            return [inproj(0, 2 * jp, 0), inproj(1, 2 * jp, 1), inproj(0, 2 * jp + 1, 2), inproj(1, 2 * jp + 1, 3)]
        if t == 2:
            return ("convs", jp)
        return ("conv", 2 * jp + (1 if t == 3 else 0))
    s -= 4 * AB
    if s < 4:
        jp = s
        return [("w_pw2", 8, (2 * jp) * 128, 128, 0), inproj(2, 2 * jp, 1),
                ("w_pw2", 8, (2 * jp + 1) * 128, 128, 2048), inproj(2, 2 * jp + 1, 3)]
    s -= 4
    if s < 8:
        hh, c = divmod(s, 4)
        h0 = 4 * hh
        if c == 0:
            return [inproj(4, h0, 0), inproj(3, h0, 1), inproj(4, h0 + 1, 2), inproj(3, h0 + 1, 3)]
        if c == 1:
            return [inproj(4, h0 + 2, 0), inproj(3, h0 + 2, 1), inproj(4, h0 + 3, 2), inproj(3, h0 + 3, 3)]
        if c == 2:
            return [inproj(6, h0 + i, i) for i in range(4)]
        return [("w_in", 8, 5 * 1024 + hh * 512, 512, 0)]
    s -= 8
    if s < 4:
        jp = s
        return [("w_out", 16, (2 * jp) * 128, 128, 0), ("w_out", 16, (2 * jp + 1) * 128, 128, 2048)]
    s -= 4
    t = s
    return [("w_pg", 8, (2 * t) * 128, 128, 0), ("w_pg", 8, (2 * t + 1) * 128, 128, 1024),
            ("w_pp", 2, (2 * t) * 128, 128, 2048), ("w_pp", 2, (2 * t + 1) * 128, 128, 2304)]


S_A = 0
S_B = 4 * AB
S_C = S_B + 4
S_D = S_C + 8
S_E = S_D + 4


def build_nc(NT=SEQ // TT, debug=False):
    ntok = NT * TT
    nc = bass.Bass("TRN2", target_bir_lowering=False)
    dram = {}
    def din(name, shape):
        dram[name] = nc.dram_tensor(name, shape, F32, kind="ExternalInput").ap()
    din("x", [ntok, D])
    din("p", [ntok, 256])
    din("w_in", [D, 7 * D])
    din("w_pw2", [D, D])
    din("w_out", [2 * D, D])
    din("w_pg", [D, D])
    din("w_pp", [256, D])
    din("fpar", [128, NPAR])
    din("ln_g", [128, D])
    din("consts", [128, NCONST])
    out_d = nc.dram_tensor("out", [ntok, D], F32, kind="ExternalOutput").ap()
    wsc = nc.dram_tensor("wsc", [NSLOT, 128, SLOTW], BF16).ap()

    P = Prog()
    es = ExitStack()
    with es:
        def sb(name, shape, dt):
            return es.enter_context(nc.sbuf_tensor("sb_" + name, shape, dt))
        for e in ("pe", "act", "dve", "pool"):
            P.sems[e] = es.enter_context(nc.semaphore("s_" + e))
        dma_names = (["ring%d" % r for r in range(NR)] + ["xa0", "xa1", "xr0", "xr1", "pt", "ob0", "ob1",
                                                            "setup", "stg0", "stg1", "stg2", "stg3", "wst0", "wst1", "wst2", "wst3"])
        for n in dma_names:
            P.sems[n] = es.enter_context(nc.semaphore("d_" + n))

        def ACT(fn, reads, writes, n=TT, tbl=None, after=None):
            nd = P.op("act", fn, reads, writes, cost=(224.0 + n) / 1.2, tbl=tbl)
            if after is not None:
                nd.preds.add(after.idx)
            return nd
        def DVE(fn, reads, writes, n=TT, accel=1.0, psum=False):
            P.op("dve", fn, reads, writes, cost=(0.95 if psum else 0.85) * (58.0 + (62.0 if psum else 0.0) + n / accel) / 0.96 * (1.0 if accel == 1.0 else 1.3))
        def POOL(fn, reads, writes, n=TT):
            P.op("pool", fn, reads, writes, cost=300.0 + 5.0 * n)
        def PE(fns, reads, writes, cols):
            P.group("pe", fns, reads, writes, cost=1.15 * (cols / 2.2 + 12.0 * len(fns)))
        SILU, LNEXP = "silu", "lnexp"

        fpar = sb("fpar", [128, NPAR], F32)
        dpar = sb("dpar", [128, NDPAR], F32)
        cst = sb("cst", [128, NCONST], F32)
        lng = sb("lng", [128, D], F32)
        ident_bf = sb("ident_bf", [128, 128], BF16)
        cmat_bf = sb("cmat_bf", [128, 128], BF16)
        onesm_bf = sb("onesm_bf", [128, 128], BF16)
        ones1k_bf = sb("ones1k_bf", [128, 128], BF16)
        amask_bf = sb("amask_bf", [128, 128], BF16)
        S = sb("S", [128, 8, 128], F32)
        S_bf = sb("S_bf", [128, 8, 128], BF16)
        vbuf = sb("vbuf", [128, NU, HIST + TT], BF16)
        ring = sb("ring", [128, NR, SLOTW], BF16)
        ps = es.enter_context(nc.psum_tensor("ps", [128, 8, 512], F32))

        identf = cst[:, K_ID:K_ID + 128]
        rmask = cst[:, K_RM:K_RM + 512]
        eps_ap = dpar[:, D_EPS:D_EPS + 1]

        def col(t, c):
            return t[:, c:c + 1]

        P.dma("sp", "setup", lambda e: e.dma_start(out=fpar[:], in_=dram["fpar"]), writes=["fpar"], nbytes=128 * NPAR * 4)
        P.dma("sp", "setup", lambda e: e.dma_start(out=cst[:], in_=dram["consts"]), writes=["cst"], nbytes=128 * NCONST * 4)
        P.dma("sp", "setup", lambda e: e.dma_start(out=lng[:], in_=dram["ln_g"]), writes=["lng"], nbytes=128 * D * 4)
        P.barrier(lambda e: e.memset(dpar[:, D_EPS + 1:D_EPS + 2], 0.0))
        DVE(lambda e: e.tensor_scalar(out=dpar[:, D_CW:D_CW + 248], in0=fpar[:, C_CW:C_CW + 248],
                                      scalar1=0.5, scalar2=None, op0=ALU.mult), ["fpar"], ["dp_cw"], n=248)
        DVE(lambda e: e.tensor_tensor(out=dpar[:, D_TMP:D_TMP + 8], in0=fpar[:, C_LB0:C_LB0 + 8],
                                      in1=fpar[:, C_LB1:C_LB1 + 8], op=ALU.subtract), ["fpar"], ["dp_tmp"], n=8)
        ACT(lambda e: e.activation(out=dpar[:, D_TL:D_TL + 8], in_=dpar[:, D_TMP:D_TMP + 8],
                                   func=AF.Tanh, scale=0.5), ["dp_tmp"], ["dp_tl"], n=8, tbl=SILU)
        DVE(lambda e: e.tensor_scalar(out=dpar[:, D_A1:D_A1 + 8], in0=dpar[:, D_TL:D_TL + 8],
                                      scalar1=-0.25, scalar2=0.25, op0=ALU.mult, op1=ALU.add), ["dp_tl"], ["dp_a1"], n=8)
        DVE(lambda e: e.tensor_scalar(out=dpar[:, D_A0:D_A0 + 8], in0=dpar[:, D_TL:D_TL + 8],
                                      scalar1=0.25, scalar2=0.75, op0=ALU.mult, op1=ALU.add), ["dp_tl"], ["dp_a0"], n=8)
        DVE(lambda e: e.tensor_scalar(out=dpar[:, D_NA1:D_NA1 + 8], in0=dpar[:, D_A1:D_A1 + 8],
                                      scalar1=-1.0, scalar2=None, op0=ALU.mult), ["dp_a1"], ["dp_na1"], n=8)
        DVE(lambda e: e.memset(dpar[:, D_EPS:D_EPS + 1], EPS), [], ["dp_eps"], n=1)
        for dst, c0 in ((ident_bf, K_ID), (cmat_bf, K_CM), (onesm_bf, K_ON), (ones1k_bf, K_O1K), (amask_bf, K_AM)):
            DVE(lambda e, dst=dst, c0=c0: e.tensor_copy(out=dst[:], in_=cst[:, c0:c0 + 128]), ["cst"], ["cbf%d" % c0], n=128)
        PE([lambda e: e.matmul(ps[:, 0, 0:8], lhsT=cst[:, K_CM:K_CM + 128], rhs=fpar[:, C_CB:C_CB + 8], start=True, stop=True)],
           ["cst", "fpar"], [("ps", 0)], cols=64)
        ACT(lambda e: e.activation(out=dpar[:, D_CBC:D_CBC + 8], in_=ps[:, 0, 0:8], func=AF.Copy), [("ps", 0)], ["dp_cbc"], n=8)
        DVE(lambda e: e.memset(S[:], 0.0), [], ["S"], n=1024)
        DVE(lambda e: e.memset(S_bf[:], 0.0), [], ["S_bf"], n=1024)
        DVE(lambda e: e.memset(vbuf[:], 0.0), [], ["vb%d" % j for j in range(NU)], n=NU * (HIST + TT))

        with nc.sbuf_tensor("sb_stg32", [128, 4, SLOTW], F32) as stg32, nc.sbuf_tensor("sb_stg16", [128, 4, SLOTW], BF16) as stg16:
            cast_engs = ("act", "dve")
            for r in range(4):
                DVE(lambda e, r=r: e.memset(stg32[:, r, :], 0.0), [], [("stg32", r, q) for q in range(4)], n=SLOTW, accel=2.0)
                DVE(lambda e, r=r: e.memset(stg16[:, r, :], 0.0), [], [("stg16", r, m) for m in range(16)], n=SLOTW, accel=4.0)
            for s in range(NSLOT):
                r = s % 4
                pieces = slot_pieces(s)
                k16 = [("stg16", r, m) for m in range(16)]
                if pieces[0] in ("conv", "convs"):
                    if pieces[0] == "conv":
                        taps = [(pieces[1], m) for m in range(16)]
                    else:
                        taps = [(2 * pieces[1], 16 + m) for m in range(NSH)] + [(2 * pieces[1] + 1, 16 + m) for m in range(NSH)]
                    for m, (u, k) in enumerate(taps):
                        if m % 2 == 0:
                            DVE(lambda e, r=r, m=m, u=u, k=k: e.tensor_scalar(out=stg16[:, r, m * 128:(m + 1) * 128], in0=cst[:, K_CM:K_CM + 128],
                                                                              scalar1=col(dpar, D_CW + u * CK + k), scalar2=None, op0=ALU.mult),
                                ["cst", "dp_cw"], [("stg16", r, m)], n=128, accel=2.0)
                        else:
                            ACT(lambda e, r=r, m=m, u=u, k=k: e.activation(out=stg16[:, r, m * 128:(m + 1) * 128], in_=cst[:, K_CM:K_CM + 128],
                                                                           func=AF.Identity, scale=col(dpar, D_CW + u * CK + k)),
                                ["cst", "dp_cw"], [("stg16", r, m)], n=128)
                    P.dma("pool", "wst%d" % r, lambda e, s=s, r=r: e.dma_start(out=wsc[s], in_=stg16[:, r, :]),
                          reads=k16, writes=[("wsc", s)], nbytes=128 * SLOTW * 2)
                    continue
                rk = []
                for q, (tn, kc, c0, ncol, off) in enumerate(pieces):
                    key = ("stg32", r, q)
                    rk.append(key)
                    src = dram[tn][:, c0:c0 + ncol].rearrange("(k p) c -> p k c", p=128)
                    dst = stg32[:, r, off:off + kc * ncol].rearrange("p (k c) -> p k c", c=ncol)
                    P.dma("sp", "stg%d" % r, lambda e, src=src, dst=dst: e.dma_start(out=dst, in_=src), writes=[key],
                          nbytes=128 * kc * ncol * 4)
                ce = cast_engs[s % 2]
                if ce == "act":
                    ACT(lambda e, r=r: e.activation(out=stg16[:, r, :], in_=stg32[:, r, :], func=AF.Copy), rk, k16, n=SLOTW)
                elif ce == "dve":
                    DVE(lambda e, r=r: e.tensor_copy(out=stg16[:, r, :], in_=stg32[:, r, :]), rk, k16, n=SLOTW, accel=2.0)
                else:
                    POOL(lambda e, r=r: e.tensor_copy(out=stg16[:, r, :], in_=stg32[:, r, :]), rk, k16, n=SLOTW)
                P.dma("pool", "wst%d" % r, lambda e, s=s, r=r: e.dma_start(out=wsc[s], in_=stg16[:, r, :]),
                      reads=k16, writes=[("wsc", s)], nbytes=128 * SLOTW * 2)
        P.barrier(lambda e: e.memset(dpar[:, D_EPS + 2:D_EPS + 3], 0.0))

        xa = sb("xa", [128, 2, D], F32)
        xr = sb("xr", [128, 2, NTB, 128], F32)
        ub = sb("ub", [128, 2, D], BF16)
        ssq = sb("ssq", [128, 8], F32)
        uT = sb("uT", [128, NU, TT], BF16)
        sigb = sb("sigb", [128, 2, TT], F32)
        ycv = sb("ycv", [128, 2, TT], F32)
        ybf = sb("ybf", [128, 2, TT], BF16)
        sqb = sb("sqb", [128, 2, TT], BF16)
        rst = sb("rst", [128, 2, TT], F32)
        yn = sb("yn", [128, 2, TT], F32)
        hn = sb("hn", [128, NU, TT], BF16)
        hTb = sb("hTb", [128, NU, TT], BF16)
        sgt = sb("sgt", [128, 1, TT], F32)
        sqbL = sb("sqbL", [128, 2, TT], BF16)
        rstL = sb("rstL", [128, 2, TT], F32)
        ynL = sb("ynL", [128, 1, TT], F32)
        rstO = sb("rstO", [128, 2, TT], F32)
        sigbL = sb("sigbL", [128, 1, TT], F32)
        ycvL = sb("ycvL", [128, 1, TT], F32)
        yT = sb("yT", [128, 2 * NU, TT], BF16)
        thf = sb("thf", [128, TT], F32)
        lfb = sb("lfb", [128, TT], F32)
        kkb = sb("kkb", [128, TT], F32)
        bbb = sb("bbb", [128, TT], F32)
        ebb = sb("ebb", [128, TT], F32)
        qT = sb("qT", [128, 4, TT], BF16)
        kT = sb("kT", [128, 4, TT], BF16)
        kdT = sb("kdT", [128, 1, TT], BF16)
        kdec = sb("kdec", [128, NTB, 4, 128], BF16)
        vtok = sb("vtok", [128, NTB, 512], BF16)
        sg = sb("sg", [128, 4, TT], BF16)
        eblast = sb("eblast", [128, 4, 8], F32)
        Am = sb("Am", [128, 1, 4, 128], BF16)
        hT = sb("hT", [128, NU, TT], F32)
        ob = sb("ob", [128, 1, NTB, 128], F32)
        pbf = sb("pbf", [128, NTB, 256], BF16)
        pT = sb("pT", [128, 2, TT], BF16)

        pt_v = xr[:].rearrange("p a b c -> p (a b c)").rearrange("p (b d) -> p b d", d=256)
        psbf = [ps[:, b, :].bitcast(BF16) for b in range(8)]
        import os as _os
        pools = {"E": [int(c) for c in _os.environ.get("BANKS_E", "0123")], "L": [int(c) for c in _os.environ.get("BANKS_L", "45")]}
        bank_rot = {"E": 0, "L": 0}
        cur_pool = ["E"]
        def newbank():
            pl = cur_pool[0]
            b = pools[pl][bank_rot[pl] % len(pools[pl])]
            bank_rot[pl] += 1
            return b
        B_ACC = 6
        B_MSQ = 7

        slot_ctr = [0]
        act_fence = [None]
        ring_owner = {}

        class RingSlot(int):
            def __new__(cls, r, n):
                o = int.__new__(cls, r)
                o.n = n
                return o

        def chk(r):
            assert ring_owner[int(r)] == r.n, "ring slot reused before its last consumer was emitted"
            return int(r)
        def load_slot(tile, s):
            n = slot_ctr[0]
            slot_ctr[0] += 1
            r = n % NR
            ring_owner[r] = n
            P.dma("sp", "ring%d" % r, lambda e: e.dma_start(out=ring[:, r, :], in_=wsc[s]),
                  reads=[("wsc", s)], writes=[("ring", r)], nbytes=128 * SLOTW * 2)
            return RingSlot(r, n)

        def mm_unit(bank, r, off, nk, rhs_fn, rkeys, extra=None, extra_cols=0):
            r = chk(r)
            fns = []
            total = nk + (len(extra) if extra else 0)
            for k in range(nk):
                fns.append(lambda e, k=k: e.matmul(ps[:, bank, :], lhsT=ring[:, r, off + k * 128: off + (k + 1) * 128],
                                                   rhs=rhs_fn(k), start=(k == 0), stop=(k == total - 1)))
            if extra:
                fns.extend(extra)
            PE(fns, [("ring", r)] + rkeys, [("ps", bank)], cols=nk * TT + extra_cols)

        def rsqrt_bank(bank, dst, dkey):
            ACT(lambda e: e.activation(out=dst, in_=ps[:, bank, :], func=AF.Ln, bias=eps_ap), [("ps", bank), "dp_eps"], [dkey], tbl=LNEXP)
            ACT(lambda e: e.activation(out=dst, in_=dst, func=AF.Exp, scale=-0.5), [dkey], [dkey], tbl=LNEXP)

        uT_keys = [("uT", tb) for tb in range(NTB)]

        def load_xa(tile, tb):
            r2 = tb % 2
            t0 = tile * TT
            P.dma("sp", "xa%d" % r2, lambda e: e.dma_start(out=xa[:, r2, :], in_=dram["x"][t0 + tb * 128:t0 + (tb + 1) * 128, :]),
                  writes=[("xa", r2)], nbytes=128 * D * 4)

        def load_p(tile):
            t0 = tile * TT
            P.dma("sp", "pt", lambda e: e.dma_start(out=pt_v, in_=dram["p"][t0:t0 + TT, :].rearrange("(b p) d -> p b d", p=128)),
                  writes=["pt", ("xr", 0), ("xr", 1)], nbytes=TT * 256 * 4)

        def prefetch_p(tile):
            load_p(tile)
            DVE(lambda e: e.tensor_copy(out=pbf[:], in_=pt_v), ["pt", ("xr", 0), ("xr", 1)], ["pbf"], n=1024, accel=2.0)

        def prefetch_x(tile):
            load_xa(tile, 0)
            load_xa(tile, 1)

        def p_chain():
            cur_pool[0] = "E"
            bk = newbank()
            PE([lambda e, tb=tb, k2=k2, bk=bk: e.transpose(out=psbf[bk][:, k2 * 512 + tb * 128: k2 * 512 + (tb + 1) * 128],
                                                           in_=pbf[:, tb, k2 * 128:(k2 + 1) * 128], identity=ident_bf[:])
                for tb in range(NTB) for k2 in range(2)], ["pbf", "cbf%d" % K_ID], [("ps", bk)], cols=1024)
            ACT(lambda e, bk=bk: e.activation(out=pT[:], in_=psbf[bk].rearrange("p (k t) -> p k t", t=512), func=AF.Copy), [("ps", bk)], ["pT"], n=1024)

        def stage_a(tile):
            t0 = tile * TT
            cur_pool[0] = "E"
            if tile > 0:
                ACT(lambda e: e.activation(out=vbuf[:, :, 0:HIST], in_=vbuf[:, :, TT:TT + HIST], func=AF.Copy),
                    ["vb%d" % j for j in range(NU)], ["vh"], n=NU * HIST)
            for tb in range(NTB):
                r2 = tb % 2
                if tb >= 2:
                    load_xa(tile, tb)
                ACT(lambda e, tb=tb, r2=r2: e.activation(out=ub[:, r2, :], in_=xa[:, r2, :], func=AF.Square, accum_out=ssq[:, tb:tb + 1]),
                    [("xa", r2)], [("ub", r2), ("ssq", tb)], n=D)
                ACT(lambda e, tb=tb: e.activation(out=ssq[:, 4 + tb:5 + tb], in_=ssq[:, tb:tb + 1], func=AF.Ln, bias=eps_ap, scale=1.0 / D),
                    [("ssq", tb), "dp_eps"], [("ssq2", tb)], n=1, tbl=LNEXP)
                ACT(lambda e, tb=tb: e.activation(out=ssq[:, 4 + tb:5 + tb], in_=ssq[:, 4 + tb:5 + tb], func=AF.Exp, scale=-0.5),
                    [("ssq2", tb)], [("ssq2", tb)], n=1, tbl=LNEXP)
                DVE(lambda e, tb=tb, r2=r2: e.scalar_tensor_tensor(out=ub[:, r2, :], in0=xa[:, r2, :], scalar=ssq[:, 4 + tb:5 + tb], in1=lng[:],
                                                                   op0=ALU.mult, op1=ALU.mult),
                    [("xa", r2), ("ssq2", tb), "lng"], [("ub", r2)], n=D)
                bk = newbank()
                PE([lambda e, k=k, r2=r2, bk=bk: e.transpose(out=psbf[bk][:, k * 128:(k + 1) * 128], in_=ub[:, r2, k * 128:(k + 1) * 128], identity=ident_bf[:])
                    for k in range(NU)], [("ub", r2), "cbf%d" % K_ID], [("ps", bk)], cols=NU * 128)
                ACT(lambda e, tb=tb, bk=bk: e.activation(out=uT[:, :, tb * 128:(tb + 1) * 128], in_=psbf[bk].rearrange("p (k t) -> p k t", t=128), func=AF.Copy),
                    [("ps", bk)], [("uT", tb)], n=D)

        for tile in range(NT):
            t0 = tile * TT
            inproj_r = [None]
            shared_r = [None]

            def conv_unit(j):
                cur_pool[0] = "E"
                jp, jj = divmod(j, 2)
                if jj == 0:
                    inproj_r[0] = load_slot(tile, S_A + AB * jp)
                r = inproj_r[0]
                rot = j % 2
                bv, bg = newbank(), newbank()
                mm_unit(bv, r, (2 * jj) * 1024, 8, lambda k: uT[:, k, :], uT_keys)
                mm_unit(bg, r, (2 * jj + 1) * 1024, 8, lambda k: uT[:, k, :], uT_keys)
                ACT(lambda e, rot=rot, bg=bg: e.activation(out=sigb[:, rot, :], in_=ps[:, bg, :], func=AF.Tanh, scale=0.5),
                    [("ps", bg)], [("sigb", rot)], tbl=SILU)
                DVE(lambda e, j=j, rot=rot, bv=bv: e.scalar_tensor_tensor(out=vbuf[:, j, HIST:HIST + TT], in0=sigb[:, rot, :], scalar=1.0, in1=ps[:, bv, :],
                                                                            op0=ALU.add, op1=ALU.mult),
                    [("sigb", rot), ("ps", bv)], ["vb%d" % j], psum=True)
                yield
                rc = load_slot(tile, S_A + AB * jp + (1 if jj == 0 else 3))
                if jj == 0:
                    shared_r[0] = load_slot(tile, S_A + AB * jp + 2)
                rsh = shared_r[0]
                b1, b2 = newbank(), newbank()
                chk(rc)
                PE([lambda e, k=k, j=j, rc=int(rc), b1=b1: e.matmul(ps[:, b1, :], lhsT=ring[:, rc, k * 128:(k + 1) * 128],
                                                                   rhs=vbuf[:, j, 2 + k:2 + k + TT], start=(k == 0), stop=False) for k in range(16)],
                   [("ring", int(rc)), "vb%d" % j, "vh"], [("ps", b1)], cols=16 * TT)
                chk(rsh)
                PE([lambda e, k=k, j=j, jj=jj, rsh=int(rsh), b1=b1: e.matmul(ps[:, b1, :], lhsT=ring[:, rsh, (jj * NSH + k - 16) * 128:(jj * NSH + k - 15) * 128],
                                                                            rhs=vbuf[:, j, 2 + k:2 + k + TT], start=False, stop=False) for k in range(16, NPE_TAPS)],
                   [("ring", int(rsh)), "vb%d" % j, "vh"], [("ps", b1)], cols=NSH * TT)
                if NPE_TAPS < CK:
                    nd = CK - NPE_TAPS
                    DVE(lambda e, j=j, rot=rot: e.tensor_scalar(out=(ybf[:, rot, :] if nd == 1 else ycv[:, rot, :]),
                                                                 in0=vbuf[:, j, 2 + NPE_TAPS:2 + NPE_TAPS + TT],
                                                                 scalar1=col(dpar, D_CW + j * CK + NPE_TAPS), scalar2=None, op0=ALU.mult),
                        ["vb%d" % j, "vh", "dp_cw"], [("ybf", rot)] if nd == 1 else [("ycv", rot)], accel=2.0)
                    for k in range(NPE_TAPS + 1, CK):
                        last = (k == CK - 1)
                        outap = ybf[:, rot, :] if last else ycv[:, rot, :]
                        DVE(lambda e, j=j, rot=rot, k=k, outap=outap: e.scalar_tensor_tensor(
                            out=outap, in0=vbuf[:, j, 2 + k:2 + k + TT], scalar=col(dpar, D_CW + j * CK + k), in1=ycv[:, rot, :],
                            op0=ALU.mult, op1=ALU.add),
                            ["vb%d" % j, "vh", ("ycv", rot)], [("ybf", rot)] if last else [("ycv", rot)])
                    PE([lambda e, rot=rot, b1=b1: e.matmul(ps[:, b1, :], lhsT=cmat_bf[:], rhs=ybf[:, rot, :], start=False, stop=True)],
                       [("ybf", rot), "cbf%d" % K_CM], [("ps", b1)], cols=TT)
                yield
                cur_pool[0] = "E"
                ACT(lambda e, j=j, rot=rot, b1=b1: e.activation(out=sqb[:, rot, :], in_=ps[:, b1, :], func=AF.Square, bias=col(dpar, D_CBC + j)),
                    [("ps", b1), "dp_cbc"], [("sqb", rot)])
                PE([lambda e, rot=rot, b2=b2: e.matmul(ps[:, b2, :], lhsT=onesm_bf[:], rhs=sqb[:, rot, :], start=True, stop=True)],
                   [("sqb", rot), "cbf%d" % K_ON], [("ps", b2)], cols=TT)
                rsqrt_bank(b2, rst[:, rot, :], ("rst", rot))
                DVE(lambda e, j=j, rot=rot, b1=b1: e.scalar_tensor_tensor(out=yn[:, rot, :], in0=ps[:, b1, :], scalar=col(dpar, D_CBC + j),
                                                                            in1=rst[:, rot, :], op0=ALU.add, op1=ALU.mult),
                    [("ps", b1), ("rst", rot), "dp_cbc"], [("yn", rot)], psum=True)
                ACT(lambda e, j=j, rot=rot: e.activation(out=hn[:, j, :], in_=yn[:, rot, :], func=AF.Silu, bias=col(fpar, C_CBETA + j),
                                                         scale=col(fpar, C_CG + j)), [("yn", rot), "fpar"], [("hn", j)], tbl=SILU)

            def conv_pair(a, b):
                ga, gb = conv_unit(a), conv_unit(b)
                for _ in range(3):
                    next(ga, None)
                    next(gb, None)

            cpre_r = [None]

            def c_pre_head(hh, i):
                cur_pool[0] = "L"
                sbase = S_C + 4 * hh
                c, ii = divmod(i, 2)
                if ii == 0:
                    cpre_r[0] = load_slot(tile, sbase + c)
                r = cpre_r[0]
                if True:
                    if True:
                        h = 4 * hh + i
                        bf_, bq_ = newbank(), newbank()
                        mm_unit(bf_, r, (2 * ii) * 1024, 8, lambda k: uT[:, k, :], uT_keys)
                        mm_unit(bq_, r, (2 * ii + 1) * 1024, 8, lambda k: uT[:, k, :], uT_keys)
                        ACT(lambda e, bf_=bf_: e.activation(out=thf[:], in_=ps[:, bf_, :], func=AF.Tanh, scale=0.5), [("ps", bf_)], ["thf"], tbl=SILU, after=act_fence[0])
                        ACT(lambda e, bq_=bq_: e.activation(out=sgt[:, 0, :], in_=ps[:, bq_, :], func=AF.Silu), [("ps", bq_)], [("sgt", 0)], tbl=SILU)
                        ACT(lambda e, h=h: e.activation(out=lfb[:], in_=thf[:], func=AF.Ln, bias=col(dpar, D_A0 + h), scale=col(dpar, D_A1 + h)),
                            ["thf", "dp_a0", "dp_a1"], ["lfb"], tbl=LNEXP)
                        DVE(lambda e, h=h: e.tensor_scalar(out=kkb[:], in0=thf[:], scalar1=col(dpar, D_NA1 + h), scalar2=col(dpar, D_A1 + h),
                                                            op0=ALU.mult, op1=ALU.add), ["thf", "dp_na1", "dp_a1"], ["kkb"], accel=2.0)
                        DVE(lambda e: e.tensor_tensor_scan(out=bbb[:], data0=rmask, data1=lfb[:], initial=0.0, op0=ALU.mult, op1=ALU.add),
                            ["lfb", "cst"], ["bbb"], n=2 * TT)
                        ACT(lambda e: e.activation(out=ebb[:], in_=bbb[:], func=AF.Exp), ["bbb"], ["ebb"], tbl=LNEXP)
                        act_fence[0] = ACT(lambda e: e.activation(out=lfb[:], in_=bbb[:], func=AF.Exp, scale=-1.0), ["bbb"], ["lfb"], tbl=LNEXP)
                        DVE(lambda e, i=i: e.tensor_copy(out=eblast[:, i, :], in_=ebb[:].rearrange("p (c s) -> p c s", s=64)[:, :, 63]),
                            ["ebb"], [("eblast", i)], n=8)
                        DVE(lambda e, i=i: e.tensor_tensor(out=qT[:, i, :], in0=sgt[:, 0, :], in1=ebb[:], op=ALU.mult), [("sgt", 0), "ebb"], [("qT", i)])
                        DVE(lambda e: e.tensor_tensor(out=kkb[:], in0=kkb[:], in1=lfb[:], op=ALU.mult), ["kkb", "lfb"], ["kkb"])
                        DVE(lambda e, i=i: e.tensor_copy(out=kT[:, i, :], in_=kkb[:]), ["kkb"], [("kT", i)], accel=2.0)
                        rk = i % 2
                        DVE(lambda e, i=i, rk=rk: e.tensor_tensor(
                            out=kdT[:, 0, :].rearrange("p (c s) -> p c s", s=64), in0=kkb[:].rearrange("p (c s) -> p c s", s=64),
                            in1=eblast[:, i, :].unsqueeze(2).to_broadcast([128, 8, 64]), op=ALU.mult),
                            ["kkb", ("eblast", i)], [("kdT", 0)])
                        bk = newbank()
                        PE([lambda e, tb=tb, rk=rk, bk=bk: e.transpose(out=psbf[bk][:, tb * 128:(tb + 1) * 128], in_=kdT[:, 0, tb * 128:(tb + 1) * 128],
                                                                       identity=ident_bf[:]) for tb in range(NTB)],
                           [("kdT", 0), "cbf%d" % K_ID], [("ps", bk)], cols=NTB * 128)
                        ACT(lambda e, i=i, bk=bk: e.activation(out=kdec[:, :, i, :], in_=psbf[bk][:, 0:512].rearrange("p (b k) -> p b k", k=128), func=AF.Copy),
                            [("ps", bk)], [("kdec", i)])

            def c_pre_tail(hh):
                cur_pool[0] = "L"
                sbase = S_C + 4 * hh
                r = load_slot(tile, sbase + 2)
                for i in range(4):
                    bk = newbank()
                    mm_unit(bk, r, i * 1024, 8, lambda k: uT[:, k, :], uT_keys)
                    ACT(lambda e, i=i, bk=bk: e.activation(out=sg[:, i, :], in_=ps[:, bk, :], func=AF.Silu), [("ps", bk)], [("sg", i)], tbl=SILU)
                r = chk(load_slot(tile, sbase + 3))
                for tb in range(NTB):
                    bk = newbank()
                    PE([lambda e, k=k, tb=tb, bk=bk, r=r: e.matmul(ps[:, bk, :], lhsT=uT[:, k, tb * 128:(tb + 1) * 128], rhs=ring[:, r, k * 512:(k + 1) * 512],
                                                                   start=(k == 0), stop=(k == 7)) for k in range(8)],
                       [("ring", r)] + uT_keys, [("ps", bk)], cols=8 * TT)
                    ACT(lambda e, tb=tb, bk=bk: e.activation(out=vtok[:, tb, :], in_=ps[:, bk, :], func=AF.Copy), [("ps", bk)], [("vtok", tb)])

            qk_keys = [("qT", i) for i in range(4)] + [("kT", i) for i in range(4)]

            def rec_tb(hh, tb):
                cur_pool[0] = "L"
                ra = tb % 2
                bA = newbank()
                PE([lambda e, i=i, tb=tb, bA=bA: e.matmul(ps[:, bA, i * 128:(i + 1) * 128], lhsT=kT[:, i, tb * 128:(tb + 1) * 128],
                                                          rhs=qT[:, i, tb * 128:(tb + 1) * 128], start=(i == 0), stop=(i == 3)) for i in range(4)],
                   qk_keys, [("ps", bA)], cols=512)
                DVE(lambda e, ra=ra, bA=bA: e.tensor_tensor(out=Am[:, 0, :, :], in0=ps[:, bA, :].rearrange("p (i t) -> p i t", t=128),
                                                             in1=amask_bf[:].unsqueeze(1).to_broadcast([128, 4, 128]), op=ALU.mult),
                    [("ps", bA), "cbf%d" % K_AM], [("Am", 0)], psum=True)
                PE([lambda e, i=i, tb=tb, ra=ra: e.matmul(ps[:, B_ACC, i * 128:(i + 1) * 128], lhsT=vtok[:, tb, i * 128:(i + 1) * 128],
                                                          rhs=Am[:, 0, i, :], start=(i == 0), stop=False) for i in range(4)],
                   [("Am", 0), ("vtok", tb)], [("ps", B_ACC)], cols=512)
                for c in range(2):
                    ch = tb * 2 + c
                    pb = c * 64
                    PE([lambda e, i=i, tb=tb, c=c, hh=hh: e.matmul(
                        ps[:, B_ACC, i * 128 + c * 64: i * 128 + c * 64 + 64], lhsT=S_bf[:, 4 * hh + i, :],
                        rhs=qT[:, i, tb * 128 + c * 64: tb * 128 + c * 64 + 64], start=False, stop=(c == 1 and i == 3)) for i in range(4)],
                       ["S_bf"] + qk_keys, [("ps", B_ACC)], cols=4 * 100)
                    bS = newbank()
                    PE([lambda e, i=i, tb=tb, pb=pb, bS=bS: e.matmul(
                        ps[:, bS, i * 128:(i + 1) * 128], lhsT=kdec[pb:pb + 64, tb, i, :], rhs=vtok[pb:pb + 64, tb, i * 128:(i + 1) * 128],
                        start=(i == 0), stop=(i == 3)) for i in range(4)],
                       [("kdec", i) for i in range(4)] + [("vtok", tb)], [("ps", bS)], cols=4 * 160)
                    for i in range(4):
                        DVE(lambda e, i=i, hh=hh, ch=ch, bS=bS: e.scalar_tensor_tensor(
                            out=S[:, 4 * hh + i, :], in0=S[:, 4 * hh + i, :], scalar=eblast[:, i, ch:ch + 1],
                            in1=ps[:, bS, i * 128:(i + 1) * 128], op0=ALU.mult, op1=ALU.add),
                            [("ps", bS), ("eblast", i), ("S", hh, i)], [("S", hh, i)], n=128, psum=True)
                    DVE(lambda e, hh=hh: e.tensor_copy(out=S_bf[:, 4 * hh:4 * hh + 4, :], in_=S[:, 4 * hh:4 * hh + 4, :]),
                        [("S", hh, i) for i in range(4)], ["S_bf"], accel=2.0)
                ro = tb % 2
                ACT(lambda e, ro=ro: e.activation(out=sqbL[:, ro, :], in_=ps[:, B_ACC, :], func=AF.Square), [("ps", B_ACC)], [("sqbL", ro)])
                bm = newbank()
                PE([lambda e, ro=ro, bm=bm: e.matmul(ps[:, bm, :], lhsT=onesm_bf[:], rhs=sqbL[:, ro, :], start=True, stop=True)],
                   [("sqbL", ro), "cbf%d" % K_ON], [("ps", bm)], cols=TT)
                rsqrt_bank(bm, rstO[:, ro, :], ("rstO", ro))
                DVE(lambda e, ro=ro: e.tensor_tensor(out=ynL[:, 0, :], in0=ps[:, B_ACC, :], in1=rstO[:, ro, :], op=ALU.mult),
                    [("ps", B_ACC), ("rstO", ro)], [("ynL", 0)], psum=True)
                for i in range(4):
                    h = 4 * hh + i
                    DVE(lambda e, i=i, h=h, tb=tb, ro=ro: e.scalar_tensor_tensor(
                        out=yT[:, NU + h, tb * 128:(tb + 1) * 128], in0=ynL[:, 0, i * 128:(i + 1) * 128], scalar=col(fpar, C_ONG + h),
                        in1=sg[:, i, tb * 128:(tb + 1) * 128], op0=ALU.mult, op1=ALU.mult),
                        [("ynL", 0), ("sg", i), "fpar"], [("yT", NU + h)], n=128)

            hn_keys = [("hn", j) for j in range(NU)]
            yT_keys = [("yT", k) for k in range(2 * NU)]
            hTb_keys = [("hTb", j) for j in range(NU)]
            bslot = [None]
            dslot = [None]

            def b_unit(j):
                cur_pool[0] = "E"
                jp, jj = divmod(j, 2)
                if jj == 0:
                    bslot[0] = load_slot(tile, S_B + jp)
                r = bslot[0]
                rot = j % 2
                b1, b2 = newbank(), newbank()
                mm_unit(b1, r, (2 * jj) * 1024, 8, lambda k: hn[:, k, :], hn_keys)
                mm_unit(b2, r, (2 * jj + 1) * 1024, 8, lambda k: uT[:, k, :], uT_keys)
                ACT(lambda e, rot=rot, b2=b2: e.activation(out=sgt[:, 0, :], in_=ps[:, b2, :], func=AF.Silu), [("ps", b2)], [("sgt", 0)], tbl=SILU)
                DVE(lambda e, j=j, rot=rot, b1=b1: e.scalar_tensor_tensor(out=yT[:, j, :], in0=ps[:, b1, :], scalar=col(fpar, C_BPW2 + j),
                                                                            in1=sgt[:, 0, :], op0=ALU.add, op1=ALU.mult),
                    [("ps", b1), ("sgt", 0), "fpar"], [("yT", j)], psum=True)

            def d_unit(tile, j):
                cur_pool[0] = "E"
                t0 = tile * TT
                jp, jj = divmod(j, 2)
                if jj == 0:
                    dslot[0] = load_slot(tile, S_D + jp)
                r = dslot[0]
                rot = j % 2
                P.dma("pool", "xr%d" % rot, lambda e, t0=t0, j=j, rot=rot: e.dma_start(
                    out=xr[:, rot, :, :], in_=dram["x"][t0:t0 + TT, j * 128:(j + 1) * 128].rearrange("(b p) d -> p b d", p=128)),
                    writes=[("xr", rot)], nbytes=TT * 128 * 4 * 2)
                bh = newbank()
                extra = [lambda e, tb=tb, rot=rot, bh=bh: e.matmul(ps[:, bh, tb * 128:(tb + 1) * 128], lhsT=xr[:, rot, tb, :], rhs=identf,
                                                                  start=False, stop=(tb == NTB - 1)) for tb in range(NTB)]
                mm_unit(bh, r, jj * 2048, 16, lambda k: yT[:, k, :], yT_keys + [("xr", rot), "cst"], extra=extra, extra_cols=4 * TT)
                ACT(lambda e, j=j, bh=bh: e.activation(out=hT[:, j, :], in_=ps[:, bh, :], func=AF.Copy), [("ps", bh)], [("hT", j)])
                ACT(lambda e, rot=rot, bh=bh: e.activation(out=sqbL[:, rot, :], in_=ps[:, bh, :], func=AF.Square), [("ps", bh)], [("sqbL", rot)])
                ACT(lambda e, j=j, bh=bh: e.activation(out=hTb[:, j, :], in_=ps[:, bh, :], func=AF.Identity, scale=col(fpar, C_PEG + j)),
                    [("ps", bh), "fpar"], [("hTb", j)])
                PE([lambda e, rot=rot, j=j: e.matmul(ps[:, B_MSQ, :], lhsT=ones1k_bf[:], rhs=sqbL[:, rot, :], start=(j == 0), stop=(j == NU - 1))],
                   [("sqbL", rot), "cbf%d" % K_O1K], [("ps", B_MSQ)], cols=TT)
                if j == NU - 1:
                    rsqrt_bank(B_MSQ, rstL[:, 0, :], ("rstL", 0))

            eslots = [None]

            def e_unit(tile, j):
                cur_pool[0] = "E"
                if j % 2 == 0:
                    eslots[0] = load_slot(tile, S_E + j // 2)
                r = eslots[0]
                rot = j % 2
                bg_, bp_ = newbank(), newbank()
                mm_unit(bg_, r, (j % 2) * 1024, 8, lambda k: hTb[:, k, :], hTb_keys)
                mm_unit(bp_, r, 2048 + (j % 2) * 256, 2, lambda k: pT[:, k, :], ["pT"])
                DVE(lambda e, rot=rot, bg_=bg_: e.tensor_tensor(out=sigbL[:, 0, :], in0=ps[:, bg_, :], in1=rstL[:, 0, :], op=ALU.mult),
                    [("ps", bg_), ("rstL", 0)], [("sigbL", 0)], psum=True)
                ACT(lambda e, rot=rot: e.activation(out=sigbL[:, 0, :], in_=sigbL[:, 0, :], func=AF.Tanh, scale=0.5), [("sigbL", 0)], [("sigbL", 0)], tbl=SILU)
                DVE(lambda e, rot=rot, bp_=bp_: e.scalar_tensor_tensor(out=sigbL[:, 0, :], in0=sigbL[:, 0, :], scalar=1.0, in1=ps[:, bp_, :],
                                                                        op0=ALU.add, op1=ALU.mult), [("sigbL", 0), ("ps", bp_)], [("sigbL", 0)], psum=True)
                DVE(lambda e, rot=rot, j=j: e.scalar_tensor_tensor(out=hT[:, j, :], in0=sigbL[:, 0, :], scalar=0.5, in1=hT[:, j, :],
                                                                    op0=ALU.mult, op1=ALU.add), [("sigbL", 0), ("hT", j)], [("hT", j)])
                ACT(lambda e, rot=rot, j=j: e.activation(out=sqbL[:, rot, :], in_=hT[:, j, :], func=AF.Square), [("hT", j)], [("sqbL", rot)])
                PE([lambda e, rot=rot, j=j: e.matmul(ps[:, B_MSQ, :], lhsT=ones1k_bf[:], rhs=sqbL[:, rot, :], start=(j == 0), stop=(j == NU - 1))],
                   [("sqbL", rot), "cbf%d" % K_O1K], [("ps", B_MSQ)], cols=TT)
                if j == NU - 1:
                    rsqrt_bank(B_MSQ, rstL[:, 1, :], ("rstL", 1))

            def e_out(tile, j):
                cur_pool[0] = "E"
                t0 = tile * TT
                rot = j % 2
                DVE(lambda e, rot=rot, j=j: e.scalar_tensor_tensor(out=ycvL[:, 0, :], in0=hT[:, j, :], scalar=col(fpar, C_FG + j), in1=rstL[:, 1, :],
                                                                    op0=ALU.mult, op1=ALU.mult), [("hT", j), ("rstL", 1), "fpar"], [("ycvL", 0)])
                bk = B_MSQ
                PE([lambda e, tb=tb, rot=rot, bk=bk: e.transpose(out=ps[:, bk, tb * 128:(tb + 1) * 128], in_=ycvL[:, 0, tb * 128:(tb + 1) * 128], identity=identf)
                    for tb in range(NTB)], [("ycvL", 0), "cst"], [("ps", bk)], cols=4 * TT)
                ACT(lambda e, rot=rot, bk=bk: e.activation(out=ob[:, 0, :, :], in_=ps[:, bk, :].rearrange("p (b d) -> p b d", d=128), func=AF.Copy),
                    [("ps", bk)], [("ob", 0)])
                P.dma("pool", "ob0", lambda e, rot=rot, j=j, t0=t0: e.dma_start(
                    out=out_d[t0:t0 + TT, j * 128:(j + 1) * 128].rearrange("(b p) d -> p b d", p=128), in_=ob[:, 0, :, :]),
                    reads=[("ob", 0)], writes=[("out", tile, j)], nbytes=TT * 128 * 4 * 2)

            if tile == 0:
                prefetch_p(0)
                prefetch_x(0)
                stage_a(0)
                p_chain()
                for jp_ in range(4):
                    conv_pair(2 * jp_, 2 * jp_ + 1)
            for t in range(4):
                c_pre_head(0, t)
                if tile > 0:
                    d_unit(tile - 1, 2 * t)
                    d_unit(tile - 1, 2 * t + 1)
            c_pre_tail(0)
            for t in range(4):
                rec_tb(0, t)
                b_unit(2 * t)
                b_unit(2 * t + 1)
            if tile + 1 < NT:
                prefetch_x(tile + 1)
            for t in range(4):
                c_pre_head(1, t)
                if tile > 0:
                    e_unit(tile - 1, 2 * t)
                    e_unit(tile - 1, 2 * t + 1)
            c_pre_tail(1)
            if tile + 1 < NT:
                stage_a(tile + 1)
            for t in range(4):
                rec_tb(1, t)
                if tile > 0:
                    e_out(tile - 1, 2 * t)
                    e_out(tile - 1, 2 * t + 1)
                if tile + 1 < NT:
                    conv_pair(2 * t, 2 * t + 1)
            if tile > 0:
                p_chain()
            if tile + 1 < NT:
                prefetch_p(tile + 1)
            if tile == NT - 1:
                for j in range(NU):
                    d_unit(tile, j)
                for j in range(NU):
                    e_unit(tile, j)
                for j in range(NU):
                    e_out(tile, j)

        P.schedule()
        build_nc.last_prog = P
        with nc.Block() as block:
            @block.sync
            def _(eng):
                P.replay("sp", eng)

            @block.scalar
            def _(eng):
                P.replay("act", eng)

            @block.vector
            def _(eng):
                P.replay("dve", eng)

            @block.gpsimd
            def _(eng):
                P.replay("pool", eng)

            @block.tensor
            def _(eng):
                P.replay("pe", eng)
    return nc


def host_consts():
    c = np.zeros((128, NCONST), np.float32)
    eye = np.eye(128, dtype=np.float32)
    c[:, K_ID:K_ID + 128] = eye
    c[:, K_CM:K_CM + 128] = eye - 1.0 / 128.0
    c[:, K_ON:K_ON + 128] = 1.0 / 128.0
    c[:, K_O1K:K_O1K + 128] = 1.0 / 1024.0
    s = np.arange(128)[:, None]
    t = np.arange(128)[None, :]
    c[:, K_AM:K_AM + 128] = ((s // 64 == t // 64) & (t >= s)).astype(np.float32)
    rm = np.ones((128, 512), np.float32)
    rm[:, ::64] = 0.0
    c[:, K_RM:K_RM + 512] = rm
    return c


def host_fpar(conv_w, conv_b, cnorm_g, cnorm_b, b_pw2, lb_logits, onorm_g, pe_norm_g, final_g):
    f = np.zeros((128, NPAR), np.float32)
    cw = np.asarray(conv_w, np.float32)[0]
    f[:, C_CW:C_CW + 248] = cw.T.reshape(NU, 128, CK).transpose(1, 0, 2).reshape(128, NU * CK)
    def fm(v):
        return np.asarray(v, np.float32).reshape(NU, 128).T
    f[:, C_CB:C_CB + 8] = fm(conv_b[0])
    f[:, C_CG:C_CG + 8] = fm(cnorm_g[0])
    f[:, C_CBETA:C_CBETA + 8] = fm(cnorm_b[0])
    f[:, C_BPW2:C_BPW2 + 8] = fm(b_pw2[0])
    f[:, C_LB0:C_LB0 + 8] = fm(lb_logits[0])
    f[:, C_LB1:C_LB1 + 8] = fm(lb_logits[1])
    f[:, C_ONG:C_ONG + 8] = fm(onorm_g[0])
    f[:, C_PEG:C_PEG + 8] = fm(pe_norm_g[0])
    f[:, C_FG:C_FG + 8] = fm(final_g)
    return f


_NC_CACHE = {}


def kernel(x, p, ln_g, w_in, conv_w, conv_b, cnorm_g, cnorm_b, w_pw2, b_pw2,
           lb_logits, onorm_g, w_out, pe_norm_g, w_pg, w_pp, final_g):
    x = np.asarray(x, np.float32)
    p = np.asarray(p, np.float32)
    B, T, _ = x.shape
    NT = T // TT
    if NT not in _NC_CACHE:
        _NC_CACHE[NT] = build_nc(NT)
    nc = _NC_CACHE[NT]
    fpar = host_fpar(conv_w, conv_b, cnorm_g, cnorm_b, b_pw2, lb_logits, onorm_g, pe_norm_g, final_g)
    consts = host_consts()
    shared = {
        "w_in": np.ascontiguousarray(np.asarray(w_in, np.float32)[0]),
        "w_pw2": np.ascontiguousarray(np.asarray(w_pw2, np.float32)[0]),
        "w_out": np.ascontiguousarray(np.asarray(w_out, np.float32)[0]),
        "w_pg": np.ascontiguousarray(np.asarray(w_pg, np.float32)[0]),
        "w_pp": np.ascontiguousarray(np.asarray(w_pp, np.float32)[0]),
        "fpar": fpar,
        "ln_g": np.ascontiguousarray(np.broadcast_to(np.asarray(ln_g, np.float32)[0].reshape(1, D), (128, D))),
        "consts": consts,
    }
    in_maps = []
    for b in range(B):
        m = dict(shared)
        m["x"] = np.ascontiguousarray(x[b])
        m["p"] = np.ascontiguousarray(p[0, b])
        in_maps.append(m)
    res = run_bass_kernel_spmd(nc, in_maps, core_ids=list(range(B)))
    return np.stack([np.asarray(r["out"], np.float32) for r in res.results], axis=0)
```
